# Optimizing a Trainium2 kernel written in Bass

```python
import math
import jax, jax.numpy as jnp
from jax import lax
import numpy as np

D_MODEL = 2048
BATCH = 2
SEQ = 4096
DEPTH = 2

D_FF = 5632
POOL_WIDTH = D_MODEL // 2
POOL_WINDOWS = (2, 4, 8, 16)
POOL_GROUP = POOL_WIDTH // len(POOL_WINDOWS)
CONV_WIDTH_CH = D_MODEL // 2
CONV_K = 3
MIX_IN = POOL_WIDTH + 3 * CONV_WIDTH_CH
MIX_OUT = POOL_WIDTH + CONV_WIDTH_CH
N_HEADS = 16
HEAD_DIM = D_MODEL // N_HEADS
Q_BLOCK = 128
LN_EPS = 1e-5
N_EVEN = (DEPTH + 1) // 2
N_ODD = DEPTH // 2
DEEPNORM_ALPHA = (2.0 * DEPTH) ** 0.25
DEEPNORM_BETA = (8.0 * DEPTH) ** -0.25

kernel_name = "hybrid_pool_shortconv_stickbreak_macaron_deepnorm"


def layer_norm(x, g, b):
    xf = x.astype(jnp.float32)
    mu = jnp.mean(xf, axis=-1, keepdims=True)
    var = jnp.mean(jnp.square(xf - mu), axis=-1, keepdims=True)
    y = (xf - mu) * lax.rsqrt(var + LN_EPS)
    return (y * g.astype(jnp.float32) + b.astype(jnp.float32)).astype(x.dtype)


def swiglu(x, w_gate, w_up, w_down):
    return (jax.nn.silu(x @ w_gate) * (x @ w_up)) @ w_down


def pool_mixer(u, w_groups, scale):
    S = u.shape[1]
    uf = u.astype(jnp.float32)
    cs = jnp.cumsum(uf, axis=1)
    count_full = jnp.arange(1, S + 1, dtype=jnp.float32)[None, :, None]
    outs = []
    for g, win in enumerate(POOL_WINDOWS):
        sl = slice(g * POOL_GROUP, (g + 1) * POOL_GROUP)
        c = cs[..., sl]
        c_prev = jnp.pad(c, ((0, 0), (win, 0), (0, 0)))[:, :S]
        mean = (c - c_prev) / jnp.minimum(count_full, float(win))
        d = (mean - uf[..., sl]).astype(u.dtype)
        outs.append(d @ w_groups[g])
    return jnp.concatenate(outs, axis=-1) * scale


def short_gated_conv(xin, gate_b, gate_c, conv_w):
    z = gate_c * xin
    ch = z.shape[-1]
    y = lax.conv_general_dilated(
        z, conv_w[:, None, :].astype(z.dtype), window_strides=(1,),
        padding=[(CONV_K - 1, 0)], dimension_numbers=("NWC", "WIO", "NWC"),
        feature_group_count=ch)
    return gate_b * y


def stick_breaking_attention(q, k, v):
    S = q.shape[2]
    scale = 1.0 / math.sqrt(HEAD_DIM)
    outs = []
    for blk in range(S // Q_BLOCK):
        q0 = blk * Q_BLOCK
        kv_len = q0 + Q_BLOCK
        qb = q[:, :, q0:kv_len]
        kb = k[:, :, :kv_len]
        vb = v[:, :, :kv_len]
        z = jnp.einsum("bhqd,bhkd->bhqk", qb, kb).astype(jnp.float32) * scale
        t_idx = q0 + jnp.arange(Q_BLOCK)[:, None]
        s_idx = jnp.arange(kv_len)[None, :]
        mask = s_idx < t_idx
        log_1m_beta = jnp.where(mask, jax.nn.log_sigmoid(-z), 0.0)
        suffix = lax.cumsum(log_1m_beta, axis=3, reverse=True) - log_1m_beta
        log_a = jax.nn.log_sigmoid(z) + suffix
        a = jnp.where(mask, jnp.exp(log_a), 0.0)
        outs.append(jnp.einsum("bhqk,bhkd->bhqd", a.astype(vb.dtype), vb))
    return jnp.concatenate(outs, axis=2)


def setup_inputs(seed: int = 0) -> dict:
    key = jax.random.key(seed)
    ks = jax.random.split(key, 16)
    f32 = jnp.float32
    nrm = lambda k, shape, s: jax.random.normal(k, shape, f32) * s
    x = jax.random.normal(ks[0], (BATCH, SEQ, D_MODEL), f32)
    ln_g = 1.0 + nrm(ks[1], (DEPTH, 3, D_MODEL), 0.02)
    ln_b = nrm(ks[2], (DEPTH, 3, D_MODEL), 0.02)
    ffn_w_gate = nrm(ks[3], (DEPTH, 2, D_MODEL, D_FF), D_MODEL ** -0.5)
    ffn_w_up = nrm(ks[4], (DEPTH, 2, D_MODEL, D_FF), D_MODEL ** -0.5)
    ffn_w_down = nrm(ks[5], (DEPTH, 2, D_FF, D_MODEL), D_FF ** -0.5 * DEEPNORM_BETA)
    mix_w_in = nrm(ks[6], (N_EVEN, D_MODEL, MIX_IN), D_MODEL ** -0.5)
    pool_w = nrm(ks[7], (N_EVEN, len(POOL_WINDOWS), POOL_GROUP, POOL_GROUP), POOL_GROUP ** -0.5)
    pool_scale = 1.0 + nrm(ks[8], (N_EVEN, POOL_WIDTH), 0.02)
    conv_w = nrm(ks[9], (N_EVEN, CONV_K, CONV_WIDTH_CH), CONV_K ** -0.5)
    mix_w_out = nrm(ks[10], (N_EVEN, MIX_OUT, D_MODEL), MIX_OUT ** -0.5 * DEEPNORM_BETA)
    attn_w_qkv = nrm(ks[11], (N_ODD, D_MODEL, 3 * D_MODEL), D_MODEL ** -0.5)
    attn_w_out = nrm(ks[12], (N_ODD, D_MODEL, D_MODEL), D_MODEL ** -0.5 * DEEPNORM_BETA)
    return {"x": x, "ln_g": ln_g, "ln_b": ln_b, "ffn_w_gate": ffn_w_gate,
            "ffn_w_up": ffn_w_up, "ffn_w_down": ffn_w_down, "mix_w_in": mix_w_in,
            "pool_w": pool_w, "pool_scale": pool_scale, "conv_w": conv_w,
            "mix_w_out": mix_w_out, "attn_w_qkv": attn_w_qkv, "attn_w_out": attn_w_out}


def reference(x, ln_g, ln_b, ffn_w_gate, ffn_w_up, ffn_w_down, mix_w_in, pool_w,
              pool_scale, conv_w, mix_w_out, attn_w_qkv, attn_w_out):
    B, S, D = x.shape
    a = DEEPNORM_ALPHA
    for layer in range(DEPTH):
        f = swiglu(x, ffn_w_gate[layer, 0], ffn_w_up[layer, 0], ffn_w_down[layer, 0])
        x = layer_norm(a * x + 0.5 * f, ln_g[layer, 0], ln_b[layer, 0])
        i = layer // 2
        if layer % 2 == 0:
            h = x @ mix_w_in[i]
            u_pool = h[..., :POOL_WIDTH]
            o0 = POOL_WIDTH
            gate_b = h[..., o0:o0 + CONV_WIDTH_CH]
            gate_c = h[..., o0 + CONV_WIDTH_CH:o0 + 2 * CONV_WIDTH_CH]
            x_conv = h[..., o0 + 2 * CONV_WIDTH_CH:]
            y_pool = pool_mixer(u_pool, pool_w[i], pool_scale[i])
            y_conv = short_gated_conv(x_conv, gate_b, gate_c, conv_w[i])
            m = jnp.concatenate([y_pool, y_conv], axis=-1) @ mix_w_out[i]
        else:
            qkv = (x @ attn_w_qkv[i]).reshape(B, S, 3, N_HEADS, HEAD_DIM)
            qkv = jnp.transpose(qkv, (2, 0, 3, 1, 4))
            o = stick_breaking_attention(qkv[0], qkv[1], qkv[2])
            o = jnp.transpose(o, (0, 2, 1, 3)).reshape(B, S, D)
            m = o @ attn_w_out[i]
        x = layer_norm(a * x + m, ln_g[layer, 1], ln_b[layer, 1])
        f = swiglu(x, ffn_w_gate[layer, 1], ffn_w_up[layer, 1], ffn_w_down[layer, 1])
        x = layer_norm(a * x + 0.5 * f, ln_g[layer, 2], ln_b[layer, 2])
    return x
```

```python
import numpy as np
import ml_dtypes
from collections import namedtuple
from contextlib import ExitStack

import concourse.bass as bass
import concourse.mybir as mybir
from concourse.bass_utils import run_bass_kernel_spmd

F32 = mybir.dt.float32
BF16 = mybir.dt.bfloat16
AF = mybir.ActivationFunctionType
ALU = mybir.AluOpType

D = 2048
DFF = 5632
T = 1024
HALO = 16
TH = T + HALO
KC = D // 128
FC = DFF // 128
NG = 4
GC = FC // NG
SEQ = 4096
ALPHA = 4.0 ** 0.25
LN_EPS = 1e-5
QSCALE = 128.0 ** -0.5
NCORES = 8

SAME_ENGINE_SYNC = True
COMPUTE = ("pe", "act", "dve", "pool")
ALL_ENG = ("pe", "act", "dve", "pool", "sp")


class Res:
    __slots__ = ("w", "r")

    def __init__(self):
        self.w = None
        self.r = {}


class DmaSem:
    __slots__ = ("count", "handle", "name")

    def __init__(self, name):
        self.count = 0
        self.handle = None
        self.name = name


class _Op:
    __slots__ = ("emit", "waits_c", "waits_d", "signal", "sig_no", "dma")

    def __init__(self, emit, waits_c, waits_d, dma):
        self.emit = emit
        self.waits_c = waits_c
        self.waits_d = waits_d
        self.signal = False
        self.sig_no = 0
        self.dma = dma


class Sched:
    def __init__(self):
        self.streams = {e: [] for e in ALL_ENG}
        self.seen_c = {e: {} for e in ALL_ENG}
        self.seen_d = {e: {} for e in ALL_ENG}
        self.dma_sems = []
        self.plan = False

    def dma_sem(self, name):
        s = DmaSem(name)
        self.dma_sems.append(s)
        return s

    def op(self, eng, emit, reads=(), writes=(), dma=None):
        if self.plan:
            return None
        deps_c = {}
        deps_d = {}

        def add(tok):
            if tok is None:
                return
            if tok[0] == "c":
                if tok[2] > deps_c.get(tok[1], -1):
                    deps_c[tok[1]] = tok[2]
            else:
                k = id(tok[1])
                if k not in deps_d or tok[2] > deps_d[k][1]:
                    deps_d[k] = (tok[1], tok[2])

        for r in reads:
            add(r.w)
        for w in writes:
            add(w.w)
            for t in w.r.values():
                add(t)
        waits_c = []
        sc = self.seen_c[eng]
        for e2, idx in deps_c.items():
            if e2 == eng and not SAME_ENGINE_SYNC:
                continue
            if idx > sc.get(e2, -1):
                sc[e2] = idx
                waits_c.append((e2, idx))
        waits_d = []
        sd = self.seen_d[eng]
        for k, (sem, val) in deps_d.items():
            if val > sd.get(k, 0):
                sd[k] = val
                waits_d.append((sem, val))
        stream = self.streams[eng]
        idx = len(stream)
        if dma is not None:
            dma.count += 16
            tok = ("d", dma, dma.count)
            key = id(dma)
        else:
            tok = ("c", eng, idx)
            key = eng
        for r in reads:
            r.r[key] = tok
        for w in writes:
            w.w = tok
            w.r = {}
        stream.append(_Op(emit, waits_c, waits_d, dma))
        return tok

    def emit(self, nc):
        for eng in ALL_ENG:
            for op in self.streams[eng]:
                for (e2, idx) in op.waits_c:
                    self.streams[e2][idx].signal = True
        for eng in ALL_ENG:
            n = 0
            for op in self.streams[eng]:
                if op.signal:
                    n += 1
                    op.sig_no = n
        with ExitStack() as st:
            csem = {e: st.enter_context(nc.semaphore("s_" + e)) for e in COMPUTE}
            for s in self.dma_sems:
                if s.count > 0:
                    s.handle = st.enter_context(nc.semaphore("d_" + s.name))
            block = st.enter_context(nc.Block())
            streams = self.streams

            def run(eng_name, engine):
                for op in streams[eng_name]:
                    for (e2, idx) in op.waits_c:
                        engine.wait_ge(csem[e2], streams[e2][idx].sig_no)
                    for (sem, val) in op.waits_d:
                        engine.wait_ge(sem.handle, val)
                    ins = op.emit(engine)
                    if op.dma is not None:
                        ins.then_inc(op.dma.handle, 16)
                    elif op.signal:
                        ins.then_inc(csem[eng_name], 1)
                last = {}
                for op in streams[eng_name]:
                    if op.dma is not None:
                        last[id(op.dma)] = op.dma
                for s in last.values():
                    engine.wait_ge(s.handle, s.count)

            @block.tensor
            def _(e):
                run("pe", e)

            @block.scalar
            def _(e):
                run("act", e)

            @block.vector
            def _(e):
                run("dve", e)

            @block.gpsimd
            def _(e):
                run("pool", e)

            @block.sync
            def _(e):
                run("sp", e)


class Slot:
    __slots__ = ("t", "res", "sem")


class WStream:
    def __init__(self, K, name, slots):
        self.K = K
        self.name = name
        self.slots = slots
        self.specs = []
        self.fences = []
        self.reset()

    def reset(self):
        self.next_use = 0
        self.next_release = 0
        self.loaded = 0
        self.fence_passed = 0
        self.extra = []

    def _load(self, i):
        slot = self.slots[i % len(self.slots)]
        for piece in self.specs[i]:
            dst_fn, src = piece
            self.K.s.op("pool", (lambda e, d=dst_fn(slot.t), s_=src: e.dma_start(out=d, in_=s_)),
                        writes=[slot.res] + self.extra, dma=slot.sem)
        self.loaded = i + 1

    def _pump(self):
        limit = self.fences[self.fence_passed] if self.fence_passed < len(self.fences) else len(self.specs)
        hi = min(len(self.specs), self.next_release + len(self.slots), limit)
        while self.loaded < hi:
            self._load(self.loaded)

    def prime(self):
        self._pump()

    def fence(self, extra_res):
        if self.K.s.plan:
            self.fences.append(len(self.specs))
            return
        self.fence_passed += 1
        self.extra = list(extra_res)
        self._pump()
        self.extra = []

    def acquire(self, spec):
        K = self.K
        if K.s.plan:
            self.specs.append(spec)
            return self.slots[0]
        i = self.next_use
        assert i < self.loaded, (self.name, i, self.loaded)
        self.next_use += 1
        return self.slots[i % len(self.slots)]

    def release(self):
        if self.K.s.plan:
            return
        self.next_release += 1
        self._pump()


TT = namedtuple("TT", "c0 n kind")
TT_H = TT(0, HALO, "h")
TT_0 = TT(HALO, 512, 0)
TT_1 = TT(HALO + 512, 512, 1)
TTS3 = (TT_H, TT_0, TT_1)
TTS2 = (TT_0, TT_1)


class Kern:
    pass


def build(stage, stop=None):
    nc = bass.Bass("TRN2", target_bir_lowering=False)
    K = Kern()
    K.nc = nc
    K.s = Sched()
    s = K.s
    doA = stage in ("A", "F")
    doB = stage in ("B", "F")
    doC = stage in ("C", "F")
    fused = stage == "F"

    def din(name, shape, dt=F32):
        return nc.dram_tensor(name, list(shape), dt, kind="ExternalInput").ap()

    _douts = {}

    def dout(name, shape, dt=F32):
        if name not in _douts:
            _douts[name] = nc.dram_tensor(name, list(shape), dt, kind="ExternalOutput").ap()
        return _douts[name]

    lng = din("lng", [128, 6, KC])
    lnb = din("lnb", [128, 6, KC])
    cst = din("cst", [128, 3, 128])
    if doA or doC:
        wgate = din("wgate", [2, 2, D, DFF])
        wup = din("wup", [2, 2, D, DFF])
        wdown = din("wdown", [2, 2, DFF, D])
    if doA:
        xT = din("xT", [D, TH])
        wmixin = din("wmixin", [D, 4096])
        wpool = din("wpool", [4, 256, 256])
        pscale = din("pscale", [128, 8])
        convw = din("convw", [128, 8, 3])
        hmask = din("hmask", [128, 1])
        invc = din("invc", [128, 4, 16])
        wmixout = din("wmixout", [D, D])
    if doB:
        wqkv = din("wqkv", [D, 3, 4, 128])
        band = din("band", [128, 896])
    if doC:
        wattout = din("wattout", [D, D])
    if fused:
        agin = nc.dram_tensor("agin", [D, T], BF16).ap()
        xg = nc.dram_tensor("xgath", [4 * D, T], BF16).ap()
        a2in = nc.dram_tensor("a2in", [4 * 512, T], BF16).ap()
        a2out = nc.dram_tensor("a2out", [4 * 512, T], BF16).ap()
        outT = dout("outT", [D, T])
    else:
        if stage == "A":
            xf_out = dout("xf_out", [D, T])
            xb_out = dout("xb_out", [D, T], BF16)
        if stage == "B":
            xg = din("xgath", [4 * D, T], BF16)
            oT = dout("oT", [512, SEQ], BF16)
        if stage == "C":
            xf_in = din("xf_in", [D, T])
            a2out = din("om", [D, T], BF16)
            outT = dout("outT", [D, T])

    base0 = (nc.sbuf_base + 31) // 32 * 32
    arena = nc.alloc_sbuf_tensor("arena", [128, (nc.sbuf_top - base0) // 32 * 32], mybir.dt.uint8)
    off = [base0]
    sb_limit = base0 + (nc.sbuf_top - base0) // 32 * 32

    def at(name, shape, dt, o=None, advance=True):
        nbytes = int(np.prod(shape[1:])) * (4 if dt == F32 else 2)
        if o is None:
            o = off[0]
            if advance:
                off[0] += (nbytes + 31) // 32 * 32
        return nc.alloc_sbuf_tensor_at(name, list(shape), dt, offset=o)

    XF = at("XF", [128, KC, TH], F32)
    o_xb = off[0]
    XB = at("XB", [128, KC, TH], BF16)
    o_w = off[0]
    wslots = []
    for i in range(4):
        sl = Slot()
        sl.t = at("W%d" % i, [128, KC, 256], BF16)
        sl.res = Res()
        sl.sem = s.dma_sem("w%d" % i)
        wslots.append(sl)
    o_rs = off[0]
    RS_SIZE = 49152
    off[0] += RS_SIZE
    AT = at("AT", [128, GC, TH], BF16, o=o_rs)
    wdslots = []
    for i in range(2):
        sl = Slot()
        sl.t = at("WD%d" % i, [128, GC, 512], BF16, o=o_rs + GC * TH * 2 + i * GC * 512 * 2)
        sl.res = Res()
        sl.sem = s.dma_sem("wd%d" % i)
        wdslots.append(sl)
    assert GC * TH * 2 + 2 * GC * 512 * 2 <= RS_SIZE
    YM = at("YM", [128, KC, T], BF16, o=o_rs)
    MT = [at("MT%d" % i, [128, TH], F32, o=o_rs + KC * T * 2 + i * TH * 4) for i in range(3)]
    assert KC * T * 2 + 3 * TH * 4 <= RS_SIZE
    QT = [at("QT%d" % i, [128, SEQ], BF16, o=o_rs + i * 24576) for i in range(2)]
    KT = [at("KT%d" % i, [128, SEQ], BF16, o=o_rs + i * 24576 + 8192) for i in range(2)]
    VV = [at("VV%d" % i, [128, 32, 128], BF16, o=o_rs + i * 24576 + 16384) for i in range(2)]
    XG = [at("XG%d" % i, [128, KC, 512], BF16, o=o_xb + i * 16384) for i in range(2)]
    o_tmp = off[0]
    off[0] += 8 * 2048
    TF = [at("TF%d" % i, [128, 512], F32, o=o_tmp + i * 2048) for i in range(8)]
    TB = [at("TB%d" % i, [128, 1024], BF16, o=o_tmp + i * 2048) for i in range(8)]
    PW = at("PW", [128, 4, 2, 256], BF16, o=o_tmp + 4 * 2048)
    rT_ = [Res() for _ in range(8)]
    rTh = [[Res(), Res()] for _ in range(8)]

    def RW(i):
        return (rT_[i], rTh[i][0], rTh[i][1])
    CST = at("CST", [128, 3, 128], BF16)
    LNG = at("LNG", [128, 6, KC], F32)
    LNB = at("LNB", [128, 6, KC], F32)
    LNAB = at("LNAB", [128, 6, KC], F32)
    EPS = at("EPS", [128, 2], F32)
    if doA:
        PSC = at("PSC", [128, 8], F32)
        CVW = at("CVW", [128, 8, 3], F32)
        HM = at("HM", [128, 1], F32)
        INVC = at("INVC", [128, 4, 16], F32)
    if doB:
        BAND = at("BAND", [128, 896], BF16)
    assert off[0] <= sb_limit, (off[0], sb_limit)
    ONES = CST[:, 0, :]
    NTRI = CST[:, 1, :]
    NONES = CST[:, 2, :]

    pb = [nc.alloc_psum_tensor("pb%d" % i, [128, 512], F32) for i in range(8)]
    rpb = [Res() for _ in range(8)]
    rhalo = [Res() for _ in range(32)]
    halo_ctr = [0]

    def halo_slot():
        i = halo_ctr[0] % 32
        halo_ctr[0] += 1
        return pb[4][:, 16 * i:16 * i + 16], rhalo[i]

    rXF = [[Res() for _ in range(3)] for _ in range(KC)]
    rXB = [[Res() for _ in range(3)] for _ in range(KC)]
    rAT = [[Res() for _ in range(3)] for _ in range(GC)]
    rYM = [[Res() for _ in range(2)] for _ in range(KC)]
    rMT = [Res() for _ in range(3)]
    rC = Res()
    rPW = Res()

    def ti(tt):
        return 0 if tt.kind == "h" else 1 + tt.kind

    def cols(tt):
        return slice(tt.c0, tt.c0 + tt.n)

    W = WStream(K, "W", wslots)
    WD = WStream(K, "WD", wdslots)

    def wspec_cols(src2d, c0, n, dst0=0):
        return ((lambda t, a=dst0, b=n: t[:, :, a:a + b]),
                src2d[:, c0:c0 + n].rearrange("(c p) f -> p c f", p=128))

    def mm_group(e, out_ap, pairs):
        n = len(pairs)
        ins = None
        for i, (l, r) in enumerate(pairs):
            ins = e.matmul(out_ap, lhsT=l, rhs=r, start=(i == 0), stop=(i == n - 1))
        return ins

    dsem_misc = [s.dma_sem("m%d" % i) for i in range(8)]
    misc_ctr = [0]

    def misc_sem():
        i = misc_ctr[0] % 8
        misc_ctr[0] += 1
        return dsem_misc[i]

    def setup():
        s.op("pool", lambda e: e.dma_start(out=CST[:], in_=cst), writes=[rC], dma=misc_sem())
        s.op("sp", lambda e: e.dma_start(out=LNG[:], in_=lng), writes=[rC], dma=misc_sem())
        s.op("sp", lambda e: e.dma_start(out=LNB[:], in_=lnb), writes=[rC], dma=misc_sem())
        s.op("dve", lambda e: e.memset(EPS[:, 0:1], LN_EPS), writes=[rC])
        s.op("dve", lambda e: e.memset(EPS[:, 1:2], 1.0), writes=[rC])
        s.op("dve", lambda e: e.tensor_scalar(out=LNAB[:], in0=LNB[:], scalar1=ALPHA, scalar2=None, op0=ALU.mult),
             reads=[rC], writes=[rC])
        if doA:
            s.op("sp", lambda e: e.dma_start(out=PSC[:], in_=pscale), writes=[rC], dma=misc_sem())
            s.op("sp", lambda e: e.dma_start(out=CVW[:], in_=convw), writes=[rC], dma=misc_sem())
            s.op("sp", lambda e: e.dma_start(out=HM[:], in_=hmask), writes=[rC], dma=misc_sem())
            s.op("sp", lambda e: e.dma_start(out=INVC[:], in_=invc), writes=[rC], dma=misc_sem())
        if doB:
            s.op("pool", lambda e: e.dma_start(out=BAND[:], in_=band), writes=[rC], dma=misc_sem())

    sctr = [0]

    def ffn(l, j, tts):
        wg = wgate[l, j]
        wu = wup[l, j]
        wd = wdown[l, j]
        cur = {}
        f2ctr = [0]
        for g in range(NG):
            for fl in range(GC):
                f = g * GC + fl
                bi, half = divmod(f, 2)
                if half == 0 or not cur:
                    cur["g"] = W.acquire([wspec_cols(wg, bi * 256, 256)])
                    cur["u"] = W.acquire([wspec_cols(wu, bi * 256, 256)])
                sg, su = cur["g"], cur["u"]
                cs = slice(half * 128, half * 128 + 128)
                for tt in tts:
                    if tt.kind == "h":
                        ph, rh = halo_slot()
                        pu, ru = halo_slot()
                    else:
                        ph, rh = pb[tt.kind][:], rpb[tt.kind]
                        pu, ru = pb[2 + tt.kind][:], rpb[2 + tt.kind]
                    xr = [rXB[c][ti(tt)] for c in range(KC)]

                    def mm(e, ph=ph, pu=pu, sg=sg, su=su, cs=cs, tt=tt):
                        mm_group(e, ph, [(sg.t[:, c, cs], XB[:, c, cols(tt)]) for c in range(KC)])
                        return mm_group(e, pu, [(su.t[:, c, cs], XB[:, c, cols(tt)]) for c in range(KC)])
                    s.op("pe", mm, reads=xr + [sg.res, su.res], writes=[rh, ru])
                    si = sctr[0] % 2
                    sctr[0] += 1
                    S = TF[si][:, 0:tt.n]
                    s.op("act", lambda e, S=S, ph=ph: e.activation(out=S, in_=ph, func=AF.Silu),
                         reads=[rh], writes=[*RW(si)])
                    s.op("dve", lambda e, S=S, pu=pu, fl=fl, tt=tt: e.tensor_tensor(
                        out=AT[:, fl, cols(tt)], in0=S, in1=pu, op=ALU.mult),
                        reads=[*RW(si), ru], writes=[rAT[fl][ti(tt)]])
                if half == 1 or f == FC - 1:
                    W.release()
                    W.release()
                    cur.clear()
            for db in range(4):
                sd = WD.acquire([((lambda t: t[:]),
                                  wd[g * GC * 128:(g + 1) * GC * 128, db * 512:(db + 1) * 512].rearrange(
                                      "(f p) d -> p f d", p=128))])
                for dl in range(4):
                    dc = db * 4 + dl
                    for tt in tts:
                        if tt.kind == "h":
                            pf, rf = halo_slot()
                        else:
                            bk = f2ctr[0] % 4
                            f2ctr[0] += 1
                            pf, rf = pb[bk][:], rpb[bk]
                        ar = [rAT[fl][ti(tt)] for fl in range(GC)]
                        s.op("pe", lambda e, pf=pf, sd=sd, dl=dl, tt=tt: mm_group(
                            e, pf, [(sd.t[:, fl, dl * 128:(dl + 1) * 128], AT[:, fl, cols(tt)]) for fl in range(GC)]),
                            reads=ar + [sd.res], writes=[rf])
                        s.op("dve", lambda e, pf=pf, dc=dc, tt=tt: e.scalar_tensor_tensor(
                            out=XF[:, dc, cols(tt)], in0=pf, scalar=0.5, in1=XF[:, dc, cols(tt)],
                            op0=ALU.mult, op1=ALU.add),
                            reads=[rf, rXF[dc][ti(tt)]], writes=[rXF[dc][ti(tt)]])
                WD.release()

    def layer_norm(li, tts, final=False):
        for tt in tts:
            layer_norm_tt(li, tt, final)

    def layer_norm_tt(li, tt, final):
        if True:
            n = tt.n
            psum_, rsum = pb[5][:, 0:n], rpb[5]
            psq_, rsq = pb[6][:, 0:n], rpb[6]
            tcol = cols(tt)
            for c in range(KC):
                hb = c % 2
                yb = TB[2][:, hb * 512:hb * 512 + n]
                sq = TB[3][:, hb * 512:hb * 512 + n]
                s.op("dve", lambda e, yb=yb, c=c: e.tensor_copy(out=yb, in_=XF[:, c, tcol]),
                     reads=[rXF[c][ti(tt)]], writes=[rTh[2][hb]])
                s.op("act", lambda e, sq=sq, c=c: e.activation(out=sq, in_=XF[:, c, tcol], func=AF.Square),
                     reads=[rXF[c][ti(tt)]], writes=[rTh[3][hb]])

                def mm(e, yb=yb, sq=sq, c=c):
                    e.matmul(psum_, lhsT=ONES, rhs=yb, start=(c == 0), stop=(c == KC - 1))
                    return e.matmul(psq_, lhsT=ONES, rhs=sq, start=(c == 0), stop=(c == KC - 1))
                wr = [rsum] if rsum is rsq else [rsum, rsq]
                s.op("pe", mm, reads=[rTh[2][hb], rTh[3][hb], rC], writes=wr)
            M = TF[4][:, 0:n]
            V = TF[1][:, 0:n]
            R = TF[5][:, 0:n]
            NM = TF[6][:, 0:n]
            s.op("dve", lambda e: e.tensor_scalar(out=M, in0=psum_, scalar1=1.0 / D, scalar2=None, op0=ALU.mult),
                 reads=[rsum], writes=[*RW(4)])
            s.op("dve", lambda e: e.tensor_tensor(out=V, in0=M, in1=M, op=ALU.mult), reads=[*RW(4)], writes=[*RW(1)])
            s.op("dve", lambda e: e.scalar_tensor_tensor(out=V, in0=psq_, scalar=1.0 / D, in1=V,
                                                         op0=ALU.mult, op1=ALU.subtract),
                 reads=[rsq, *RW(1)], writes=[*RW(1)])
            s.op("act", lambda e: e.activation(out=V, in_=V, func=AF.Ln, bias=EPS[:, 0:1], scale=1.0),
                 reads=[*RW(1), rC], writes=[*RW(1)])
            s.op("act", lambda e: e.activation(out=R, in_=V, func=AF.Exp, scale=-0.5), reads=[*RW(1)], writes=[*RW(5)])
            s.op("dve", lambda e: e.scalar_tensor_tensor(out=NM, in0=M, scalar=-1.0, in1=R, op0=ALU.mult, op1=ALU.mult),
                 reads=[*RW(4), *RW(5)], writes=[*RW(6)])
            for c in range(KC):
                tsl = 7 if c % 2 == 0 else 0
                t1 = TF[tsl][:, 0:n]
                gc = LNG[:, li, c:c + 1]
                s.op("dve", lambda e, t1=t1, c=c, gc=gc: e.scalar_tensor_tensor(
                    out=t1, in0=XF[:, c, tcol], scalar=gc, in1=R, op0=ALU.mult, op1=ALU.mult),
                    reads=[rXF[c][ti(tt)], *RW(5), rC], writes=[*RW(tsl)])
                s.op("dve", lambda e, t1=t1, gc=gc: e.scalar_tensor_tensor(
                    out=t1, in0=NM, scalar=gc, in1=t1, op0=ALU.mult, op1=ALU.add),
                    reads=[*RW(6), *RW(tsl), rC], writes=[*RW(tsl)])
                if final:
                    s.op("act", lambda e, t1=t1, c=c: e.activation(
                        out=XF[:, c, tcol], in_=t1, func=AF.Identity, bias=LNB[:, li, c:c + 1], scale=1.0),
                        reads=[*RW(tsl), rC], writes=[rXF[c][ti(tt)]])
                else:
                    s.op("act", lambda e, t1=t1, c=c: e.activation(
                        out=XB[:, c, tcol], in_=t1, func=AF.Identity, bias=LNB[:, li, c:c + 1], scale=1.0),
                        reads=[*RW(tsl), rC], writes=[rXB[c][ti(tt)]])
                    s.op("act", lambda e, t1=t1, c=c: e.activation(
                        out=XF[:, c, tcol], in_=t1, func=AF.Identity, bias=LNAB[:, li, c:c + 1], scale=ALPHA),
                        reads=[*RW(tsl), rC], writes=[rXF[c][ti(tt)]])

    def proj_chunk(slot, cs, tts, want_halo=True):
        outs = []
        pair = proj_ctr[0] % 2
        proj_ctr[0] += 1
        for tt in tts:
            if tt.kind == "h":
                p, r = halo_slot()
            else:
                b = pair * 2 + tt.kind
                p, r = pb[b][:], rpb[b]
            xr = [rXB[c][ti(tt)] for c in range(KC)]
            s.op("pe", lambda e, p=p, slot=slot, cs=cs, tt=tt: mm_group(
                e, p, [(slot.t[:, c, cs], XB[:, c, cols(tt)]) for c in range(KC)]),
                reads=xr + [slot.res], writes=[r])
            outs.append((tt, p, r))
        return outs

    proj_ctr = [0]

    def mixer0():
        mw = wmixin
        s.op("pool", lambda e: e.dma_start(out=PW[:], in_=wpool.rearrange("g (i p) c -> p g i c", p=128)),
             reads=[], writes=[rPW, *RW(4), *RW(5)], dma=misc_sem())
        U, A_, B_ = MT[0], MT[1], MT[2]
        rU, rA, rB = rMT
        for g in range(4):
            win = 2 ** (g + 1)
            slot = W.acquire([wspec_cols(mw, 256 * g, 256)])
            for i in range(2):
                ch = 2 * g + i
                outs = proj_chunk(slot, slice(i * 128, i * 128 + 128), TTS3)
                for (tt, p, r) in outs:
                    if tt.kind == "h":
                        s.op("act", lambda e, p=p: e.activation(out=U[:, 0:16], in_=p, func=AF.Identity,
                                                                scale=HM[:, 0:1]),
                             reads=[r, rC], writes=[rU])
                    else:
                        s.op("act", lambda e, p=p, tt=tt: e.activation(out=U[:, cols(tt)], in_=p, func=AF.Identity),
                             reads=[r], writes=[rU])
                src, rsrc = U, rU
                st = 1
                pp = [(A_, rA), (B_, rB)]
                k = 0
                while st < win:
                    dst, rdst = pp[k % 2]
                    lo = 2 * st - 1
                    s.op("dve", lambda e, dst=dst, src=src, lo=lo, st=st: e.tensor_tensor(
                        out=dst[:, lo:TH], in0=src[:, lo:TH], in1=src[:, lo - st:TH - st], op=ALU.add),
                        reads=[rsrc], writes=[rdst])
                    src, rsrc = dst, rdst
                    st *= 2
                    k += 1
                DM = TB[2 + i]
                s.op("dve", lambda e, src=src, DM=DM, win=win: e.scalar_tensor_tensor(
                    out=DM[:, 0:T], in0=src[:, HALO:TH], scalar=1.0 / win, in1=U[:, HALO:TH],
                    op0=ALU.mult, op1=ALU.subtract),
                    reads=[rsrc, rU], writes=[*RW(2 + i)])
                oth, roth = pp[k % 2]
                s.op("dve", lambda e, src=src, oth=oth, g=g: e.tensor_tensor(
                    out=oth[:, 0:16], in0=src[:, HALO:HALO + 16], in1=INVC[:, g, :], op=ALU.mult),
                    reads=[rsrc, rC], writes=[roth])
                s.op("dve", lambda e, oth=oth, DM=DM: e.tensor_tensor(
                    out=DM[:, 0:16], in0=oth[:, 0:16], in1=U[:, HALO:HALO + 16], op=ALU.subtract),
                    reads=[roth, rU], writes=[*RW(2 + i)])
            W.release()
            for jj in range(2):
                ch = 2 * g + jj
                for tt in TTS2:
                    b = 5 + tt.kind
                    s.op("pe", lambda e, b=b, g=g, jj=jj, tt=tt: mm_group(
                        e, pb[b][:], [(PW[:, g, i2, jj * 128:(jj + 1) * 128],
                                       TB[2 + i2][:, tt.c0 - HALO:tt.c0 - HALO + 512]) for i2 in range(2)]),
                        reads=[rPW, *RW(2), *RW(3)], writes=[rpb[b]])
                    s.op("act", lambda e, b=b, ch=ch, tt=tt: e.activation(
                        out=YM[:, ch, tt.c0 - HALO:tt.c0 - HALO + 512], in_=pb[b][:], func=AF.Identity,
                        scale=PSC[:, ch:ch + 1]),
                        reads=[rpb[b], rC], writes=[rYM[ch][tt.kind]])
        for ip in range(4):
            slC = W.acquire([wspec_cols(mw, 2048 + 256 * ip, 256)])
            slX = W.acquire([wspec_cols(mw, 3072 + 256 * ip, 256)])
            slB = W.acquire([wspec_cols(mw, 1024 + 256 * ip, 256)])
            for i in range(2):
                ci = 2 * ip + i
                cs = slice(i * 128, i * 128 + 128)
                Csb, rCs = MT[0], rMT[0]
                Z, rZ = MT[1], rMT[1]
                ACC, rACC = MT[2], rMT[2]
                for (tt, p, r) in proj_chunk(slC, cs, TTS3):
                    if tt.kind == "h":
                        s.op("act", lambda e, p=p: e.activation(out=Csb[:, 0:16], in_=p, func=AF.Identity,
                                                                scale=HM[:, 0:1]),
                             reads=[r, rC], writes=[rCs])
                    else:
                        s.op("act", lambda e, p=p, tt=tt: e.activation(out=Csb[:, cols(tt)], in_=p, func=AF.Identity),
                             reads=[r], writes=[rCs])
                for (tt, p, r) in proj_chunk(slX, cs, TTS3):
                    s.op("dve", lambda e, p=p, tt=tt: e.tensor_tensor(
                        out=Z[:, cols(tt)], in0=Csb[:, cols(tt)], in1=p, op=ALU.mult),
                        reads=[r, rCs], writes=[rZ])
                s.op("dve", lambda e, ci=ci: e.tensor_scalar(
                    out=ACC[:, 0:T], in0=Z[:, HALO:TH], scalar1=CVW[:, ci, 2:3], scalar2=None, op0=ALU.mult),
                    reads=[rZ, rC], writes=[rACC])
                s.op("dve", lambda e, ci=ci: e.scalar_tensor_tensor(
                    out=ACC[:, 0:T], in0=Z[:, HALO - 1:TH - 1], scalar=CVW[:, ci, 1:2], in1=ACC[:, 0:T],
                    op0=ALU.mult, op1=ALU.add),
                    reads=[rZ, rC, rACC], writes=[rACC])
                s.op("dve", lambda e, ci=ci: e.scalar_tensor_tensor(
                    out=ACC[:, 0:T], in0=Z[:, HALO - 2:TH - 2], scalar=CVW[:, ci, 0:1], in1=ACC[:, 0:T],
                    op0=ALU.mult, op1=ALU.add),
                    reads=[rZ, rC, rACC], writes=[rACC])
                for (tt, p, r) in proj_chunk(slB, cs, TTS2):
                    s.op("dve", lambda e, p=p, tt=tt, ci=ci: e.tensor_tensor(
                        out=YM[:, 8 + ci, tt.c0 - HALO:tt.c0 - HALO + 512],
                        in0=ACC[:, tt.c0 - HALO:tt.c0 - HALO + 512], in1=p, op=ALU.mult),
                        reads=[r, rACC], writes=[rYM[8 + ci][tt.kind]])
            W.release()
            W.release()
            W.release()
        out_proj(wmixout, YM, rYM)

    def out_proj(wsrc, Y, rY):
        for ob in range(8):
            slot = W.acquire([wspec_cols(wsrc, 256 * ob, 256)])
            for i in range(2):
                dc = 2 * ob + i
                pair = proj_ctr[0] % 2
                proj_ctr[0] += 1
                for tt in TTS2:
                    b = pair * 2 + tt.kind
                    s.op("pe", lambda e, b=b, slot=slot, i=i, tt=tt: mm_group(
                        e, pb[b][:], [(slot.t[:, c, i * 128:(i + 1) * 128],
                                       Y[:, c, tt.c0 - HALO:tt.c0 - HALO + 512]) for c in range(KC)]),
                        reads=[rY[c][tt.kind] for c in range(KC)] + [slot.res], writes=[rpb[b]])
                    s.op("dve", lambda e, b=b, dc=dc, tt=tt: e.tensor_tensor(
                        out=XF[:, dc, cols(tt)], in0=pb[b][:], in1=XF[:, dc, cols(tt)], op=ALU.add),
                        reads=[rpb[b], rXF[dc][ti(tt)]], writes=[rXF[dc][ti(tt)]])
            W.release()

    rXG = [Res(), Res()]
    xg_sem = [s.dma_sem("xg0"), s.dma_sem("xg1")]
    rQ = [[Res() for _ in range(8)] for _ in range(2)]
    rK = [[Res() for _ in range(8)] for _ in range(2)]
    rV = [[Res() for _ in range(8)] for _ in range(2)]
    xg_ctr = [0]
    rXGdram = Res()
    rOdram = Res()
    osem = [s.dma_sem("o0"), s.dma_sem("o1")]

    def proj_tile(hl, j, slQK, slV):
        buf = hl % 2
        xi = xg_ctr[0] % 2
        xg_ctr[0] += 1
        xs = XG[xi]
        src = xg[(j // 2) * D:(j // 2 + 1) * D, (j % 2) * 512:(j % 2) * 512 + 512].rearrange("(c p) t -> p c t", p=128)
        s.op("sp", lambda e, xs=xs, src=src: e.dma_start(out=xs[:], in_=src),
             reads=[rXGdram], writes=[rXG[xi]] + ([rXB[c][k] for c in range(KC) for k in range(3)] if xg_ctr[0] <= 2 else []),
             dma=xg_sem[xi])
        s.op("pe", lambda e, xs=xs: mm_group(e, pb[6][:], [(slQK.t[:, c, 0:128], xs[:, c, :]) for c in range(KC)]),
             reads=[rXG[xi], slQK.res], writes=[rpb[6]])
        s.op("act", lambda e: e.activation(out=QT[buf][:, j * 512:(j + 1) * 512], in_=pb[6][:], func=AF.Identity,
                                           scale=QSCALE),
             reads=[rpb[6]], writes=[rQ[buf][j]])
        s.op("pe", lambda e, xs=xs: mm_group(e, pb[7][:], [(slQK.t[:, c, 128:256], xs[:, c, :]) for c in range(KC)]),
             reads=[rXG[xi], slQK.res], writes=[rpb[7]])
        s.op("dve", lambda e: e.tensor_copy(out=KT[buf][:, j * 512:(j + 1) * 512], in_=pb[7][:]),
             reads=[rpb[7]], writes=[rK[buf][j]])

        def mmv(e, xs=xs):
            ins = None
            for blk in range(4):
                ins = mm_group(e, pb[6][:, blk * 128:(blk + 1) * 128],
                               [(xs[:, c, blk * 128:(blk + 1) * 128], slV.t[:, c, 0:128]) for c in range(KC)])
            return ins
        s.op("pe", mmv, reads=[rXG[xi], slV.res], writes=[rpb[6]])
        s.op("dve", lambda e: e.tensor_copy(out=VV[buf][:, 4 * j:4 * j + 4, :],
                                            in_=pb[6][:].rearrange("p (b d) -> p b d", b=4)),
             reads=[rpb[6]], writes=[rV[buf][j]])

    att_ctr = [0]

    def attn_tile(hl, jq):
        buf = hl % 2
        qs = QT[buf][:, jq * 512:(jq + 1) * 512]
        ob = 4 + (jq % 2)
        first = True
        nkb = 4 * jq + 4
        cb_cur = None
        for kb in range(nkb - 1, -1, -1):
            it = att_ctr[0]
            att_ctr[0] += 1
            zb = it % 2
            lb = 2 + (it % 2)
            diag = kb >= 4 * jq
            dd = kb - 4 * jq
            ks = KT[buf][:, kb * 128:(kb + 1) * 128]
            kj = kb // 4
            mask = BAND[:, 384 - 128 * dd:384 - 128 * dd + 512] if diag else None
            E = TF[zb]
            SP = TB[2][:, zb * 512:zb * 512 + 512]
            rSP = rTh[2][zb]
            A = TB[4][:, zb * 512:zb * 512 + 512]
            rA = rTh[4][zb]
            s.op("pe", lambda e, zb=zb, ks=ks: e.matmul(pb[zb][:], lhsT=ks, rhs=qs, start=True, stop=True),
                 reads=[rK[buf][kj], rQ[buf][jq]], writes=[rpb[zb]])
            s.op("act", lambda e, zb=zb, E=E: e.activation(out=E[:], in_=pb[zb][:], func=AF.Exp),
                 reads=[rpb[zb]], writes=[*RW(zb)])
            s.op("act", lambda e, E=E, SP=SP: e.activation(out=SP, in_=E[:], func=AF.Ln, bias=EPS[:, 1:2], scale=1.0),
                 reads=[*RW(zb)], writes=[rSP])
            if diag:
                s.op("dve", lambda e, SP=SP, mask=mask: e.tensor_tensor(out=SP, in0=SP, in1=mask, op=ALU.mult),
                     reads=[rSP, rC], writes=[rSP])

            def mml(e, lb=lb, ks=ks, SP=SP, cb_cur=cb_cur, first=first):
                e.matmul(pb[lb][:], lhsT=ks, rhs=qs, start=True, stop=False)
                ins = e.matmul(pb[lb][:], lhsT=NTRI, rhs=SP, start=False, stop=first)
                if not first:
                    ins = e.matmul(pb[lb][:], lhsT=NONES, rhs=TB[3][:, cb_cur * 512:cb_cur * 512 + 512],
                                   start=False, stop=True)
                return ins
            rd = [rK[buf][kj], rQ[buf][jq], rSP, rC]
            if not first:
                rd.append(rTh[3][cb_cur])
            s.op("pe", mml, reads=rd, writes=[rpb[lb]])
            if kb > 0:
                if first:
                    s.op("dve", lambda e, SP=SP: e.tensor_copy(out=TB[3][:, 0:512], in_=SP),
                         reads=[rSP], writes=[rTh[3][0]])
                    cb_cur = 0
                else:
                    nx = 1 - cb_cur
                    s.op("dve", lambda e, SP=SP, cb_cur=cb_cur, nx=nx: e.tensor_tensor(
                        out=TB[3][:, nx * 512:nx * 512 + 512], in0=TB[3][:, cb_cur * 512:cb_cur * 512 + 512],
                        in1=SP, op=ALU.add),
                        reads=[rSP, rTh[3][cb_cur]], writes=[rTh[3][nx]])
                    cb_cur = nx
            s.op("act", lambda e, lb=lb, A=A: e.activation(out=A, in_=pb[lb][:], func=AF.Exp),
                 reads=[rpb[lb]], writes=[rA])
            if diag:
                s.op("dve", lambda e, A=A, mask=mask: e.tensor_tensor(out=A, in0=A, in1=mask, op=ALU.mult),
                     reads=[rA, rC], writes=[rA])
            s.op("pe", lambda e, A=A, kb=kb, first=first: e.matmul(
                pb[ob][:], lhsT=VV[buf][:, kb, :], rhs=A, start=first, stop=(kb == 0)),
                reads=[rV[buf][kj], rA], writes=[rpb[ob]])
            first = False
        osl = jq % 2
        OS = TB[5][:, osl * 512:osl * 512 + 512]
        s.op("dve", lambda e, OS=OS: e.tensor_copy(out=OS, in_=pb[ob][:]), reads=[rpb[ob]], writes=[rTh[5][osl]])
        if fused:
            r_ = jq // 2
            dst = a2in[r_ * 512 + hl * 128:r_ * 512 + (hl + 1) * 128, (jq % 2) * 512:(jq % 2) * 512 + 512]
        else:
            dst = oT[hl * 128:(hl + 1) * 128, jq * 512:(jq + 1) * 512]
        s.op("sp", lambda e, OS=OS, dst=dst: e.dma_start(out=dst, in_=OS), reads=[rTh[5][osl]], writes=[rOdram],
             dma=osem[osl])

    def attention():
        sl = {}

        def get_w(hl):
            a = W.acquire([((lambda t: t[:, :, 0:128]), wqkv[:, 0, hl, :].rearrange("(c p) f -> p c f", p=128)),
                           ((lambda t: t[:, :, 128:256]), wqkv[:, 1, hl, :].rearrange("(c p) f -> p c f", p=128))])
            b = W.acquire([((lambda t: t[:, :, 0:128]), wqkv[:, 2, hl, :].rearrange("(c p) f -> p c f", p=128))])
            return a, b
        a, b = get_w(0)
        for j in range(8):
            proj_tile(0, j, a, b)
        W.release()
        W.release()
        for hl in range(4):
            if hl + 1 < 4:
                a, b = get_w(hl + 1)
            for jq in range(8):
                attn_tile(hl, jq)
                if hl + 1 < 4:
                    proj_tile(hl + 1, jq, a, b)
            if hl + 1 < 4:
                W.release()
                W.release()

    cc_sem = [s.dma_sem("cc0"), s.dma_sem("cc1")]
    GROUPS = [[0, 1, 2, 3], [4, 5, 6, 7]]

    def program():
        setup()
        allxf = [rXF[c][k] for c in range(KC) for k in range(3)]
        allxb = [rXB[c][k] for c in range(KC) for k in range(3)]
        if doA:
            s.op("sp", lambda e: e.dma_start(out=XF[:], in_=xT.rearrange("(c p) t -> p c t", p=128)),
                 writes=allxf, dma=misc_sem())
            for c in range(KC):
                s.op("dve", lambda e, c=c: e.tensor_copy(out=XB[:, c, :], in_=XF[:, c, :]),
                     reads=rXF[c], writes=rXB[c])
                s.op("act", lambda e, c=c: e.activation(out=XF[:, c, :], in_=XF[:, c, :], func=AF.Identity, scale=ALPHA),
                     reads=rXF[c] + rXB[c], writes=rXF[c])
            def dbg():
                dx = dout("dbg_xf", [D, TH])
                db = dout("dbg_xb", [D, TH], BF16)
                s.op("sp", lambda e: e.dma_start(out=dx.rearrange("(c p) t -> p c t", p=128), in_=XF[:]),
                     reads=allxf, dma=misc_sem())
                s.op("sp", lambda e: e.dma_start(out=db.rearrange("(c p) t -> p c t", p=128), in_=XB[:]),
                     reads=allxb, dma=misc_sem())
            if stop == "init":
                return dbg()
            ffn(0, 0, TTS3)
            if stop == "ffn":
                return dbg()
            layer_norm(0, TTS3)
            if stop == "ln0":
                return dbg()
            mixer0()
            if stop == "mixer":
                return dbg()
            def dump(name):
                if stop != "all":
                    return
                dx = dout("dump_" + name, [D, TH])
                s.op("sp", lambda e: e.dma_start(out=dx.rearrange("(c p) t -> p c t", p=128), in_=XF[:]),
                     reads=allxf, dma=misc_sem())
            WD.fence([r for c in range(KC) for r in rYM[c]] + rMT)
            layer_norm(1, TTS2)
            dump("ln1")
            ffn(0, 1, TTS2)
            dump("ffn01")
            layer_norm(2, TTS2)
            dump("ln2")
            ffn(1, 0, TTS2)
            dump("ffn10")
            layer_norm(3, TTS2)
            if fused:
                s.op("sp", lambda e: e.dma_start(out=agin.rearrange("(c p) t -> p c t", p=128), in_=XB[:, :, HALO:TH]),
                     reads=allxb, writes=[rXGdram], dma=misc_sem())
                s.op("pool", lambda e: e.collective_compute("AllGather", ALU.bypass, replica_groups=GROUPS,
                                                            ins=[agin], outs=[xg]),
                     reads=[rXGdram], writes=[rXGdram], dma=cc_sem[0])
            else:
                s.op("sp", lambda e: e.dma_start(out=xb_out.rearrange("(c p) t -> p c t", p=128), in_=XB[:, :, HALO:TH]),
                     reads=allxb, dma=misc_sem())
                s.op("sp", lambda e: e.dma_start(out=xf_out.rearrange("(c p) t -> p c t", p=128), in_=XF[:, :, HALO:TH]),
                     reads=allxf, dma=misc_sem())
        if doB:
            attention()
        if doC:
            OM = XB[:, :, HALO:TH]
            rOM = [[rXB[c][1], rXB[c][2]] for c in range(KC)]
            if fused:
                s.op("pool", lambda e: e.collective_compute("AllToAll", ALU.bypass, replica_groups=GROUPS,
                                                            ins=[a2in], outs=[a2out]),
                     reads=[rOdram], writes=[rOdram], dma=cc_sem[1])
            else:
                s.op("sp", lambda e: e.dma_start(out=XF[:, :, HALO:TH], in_=xf_in.rearrange("(c p) t -> p c t", p=128)),
                     writes=allxf, dma=misc_sem())
            s.op("sp", lambda e: e.dma_start(out=OM, in_=a2out.rearrange("(c p) t -> p c t", p=128)),
                 reads=[rOdram], writes=allxb + rXG, dma=misc_sem())

            class _Y:
                def __getitem__(self, idx):
                    p, c, ts = idx
                    return XB[:, c, HALO + ts.start:HALO + ts.stop]
            WD.fence([r for b_ in range(2) for j_ in range(8) for r in (rQ[b_][j_], rK[b_][j_], rV[b_][j_])])
            out_proj(wattout, _Y(), rOM)
            layer_norm(4, TTS2)
            ffn(1, 1, TTS2)
            layer_norm(5, TTS2, final=True)
            for tt in TTS2:
                s.op("sp", lambda e, tt=tt: e.dma_start(
                    out=outT.rearrange("(c p) t -> p c t", p=128)[:, :, tt.c0 - HALO:tt.c0 - HALO + 512],
                    in_=XF[:, :, cols(tt)]),
                    reads=[rXF[c][ti(tt)] for c in range(KC)], dma=misc_sem())

    s.plan = True
    program()
    s.plan = False
    W.reset()
    WD.reset()
    halo_ctr[0] = 0
    sctr[0] = 0
    proj_ctr[0] = 0
    xg_ctr[0] = 0
    att_ctr[0] = 0
    misc_ctr[0] = 0
    W.prime()
    WD.prime()
    program()
    s.emit(nc)
    return nc


_NC_CACHE = {}


def _get_nc(stage):
    if stage not in _NC_CACHE:
        _NC_CACHE[stage] = build(stage)
    return _NC_CACHE[stage]


def _consts():
    c = np.zeros((128, 3, 128), np.float32)
    c[:, 0, :] = 1.0
    jp = np.arange(128)[:, None]
    j = np.arange(128)[None, :]
    c[:, 1, :] = -(jp >= j).astype(np.float32)
    c[:, 2, :] = -1.0
    y = np.arange(896)[None, :]
    p = np.arange(128)[:, None]
    band = ((y - 384) > p).astype(np.float32)
    return c, band


def _common_maps(inp):
    cst, band = _consts()
    lng = np.ascontiguousarray(inp["ln_g"].reshape(6, KC, 128).transpose(2, 0, 1))
    lnb = np.ascontiguousarray(inp["ln_b"].reshape(6, KC, 128).transpose(2, 0, 1))
    return dict(cst=cst, lng=lng, lnb=lnb), band


def _maps_A(inp, common):
    x = inp["x"]
    maps = []
    pscale = np.ascontiguousarray(inp["pool_scale"][0].reshape(8, 128).T)
    convw = np.ascontiguousarray(inp["conv_w"][0].reshape(3, 8, 128).transpose(2, 1, 0))
    for core in range(NCORES):
        b, r = divmod(core, 4)
        xt = np.zeros((D, TH), np.float32)
        lo = r * T
        xt[:, HALO:] = x[b, lo:lo + T].T
        if r > 0:
            xt[:, :HALO] = x[b, lo - HALO:lo].T
        invc = np.zeros((128, 4, 16), np.float32)
        for g in range(4):
            win = 2 ** (g + 1)
            if r == 0:
                invc[:, g, :] = 1.0 / np.minimum(np.arange(1, 17), win)
            else:
                invc[:, g, :] = 1.0 / win
        m = dict(common)
        m.update(xT=xt, wgate=inp["ffn_w_gate"], wup=inp["ffn_w_up"], wdown=inp["ffn_w_down"],
                 wmixin=inp["mix_w_in"][0], wpool=inp["pool_w"][0], pscale=pscale, convw=convw,
                 hmask=np.full((128, 1), 0.0 if r == 0 else 1.0, np.float32), invc=invc,
                 wmixout=inp["mix_w_out"][0])
        maps.append(m)
    return maps


def _wqkv_core(inp, r):
    w = inp["attn_w_qkv"][0].reshape(D, 3, 16, 128)
    return np.ascontiguousarray(w[:, :, 4 * r:4 * r + 4, :])


def kernel(**inp):
    inp = {k: np.asarray(v) for k, v in inp.items()}
    common, band = _common_maps(inp)
    cores = list(range(NCORES))
    resA = run_bass_kernel_spmd(_get_nc("A"), _maps_A(inp, common), core_ids=cores).results
    mapsB = []
    for core in cores:
        b, r = divmod(core, 4)
        xg = np.concatenate([resA[4 * b + rr]["xb_out"] for rr in range(4)], axis=0)
        m = dict(common)
        m.update(xgath=xg, wqkv=_wqkv_core(inp, r), band=band)
        mapsB.append(m)
    resB = run_bass_kernel_spmd(_get_nc("B"), mapsB, core_ids=cores).results
    mapsC = []
    for core in cores:
        b, r = divmod(core, 4)
        om = np.concatenate([resB[4 * b + rr]["oT"][:, r * T:(r + 1) * T] for rr in range(4)], axis=0)
        m = dict(common)
        m.update(xf_in=resA[core]["xf_out"], om=np.ascontiguousarray(om), wgate=inp["ffn_w_gate"],
                 wup=inp["ffn_w_up"], wdown=inp["ffn_w_down"], wattout=inp["attn_w_out"][0])
        mapsC.append(m)
    resC = run_bass_kernel_spmd(_get_nc("C"), mapsC, core_ids=cores).results
    out = np.empty((2, SEQ, D), np.float32)
    for core in cores:
        b, r = divmod(core, 4)
        out[b, r * T:(r + 1) * T] = resC[core]["outT"].T
    return out
```

```python
import numpy as np
import ml_dtypes
from collections import namedtuple
from contextlib import ExitStack

import concourse.bass as bass
import concourse.mybir as mybir
from concourse.bass_utils import run_bass_kernel_spmd

F32 = mybir.dt.float32
BF16 = mybir.dt.bfloat16
AF = mybir.ActivationFunctionType
ALU = mybir.AluOpType

D = 2048
DFF = 5632
T = 1024
HALO = 16
TH = T + HALO
KC = D // 128
FC = DFF // 128
NG = 4
GC = FC // NG
SEQ = 4096
ALPHA = 4.0 ** 0.25
LN_EPS = 1e-5
QSCALE = 128.0 ** -0.5
NCORES = 8

SAME_ENGINE_SYNC = True
COMPUTE = ("pe", "act", "dve", "pool")
ALL_ENG = ("pe", "act", "dve", "pool", "sp")


class Res:
    __slots__ = ("w", "r")

    def __init__(self):
        self.w = None
        self.r = {}


class DmaSem:
    __slots__ = ("count", "handle", "name")

    def __init__(self, name):
        self.count = 0
        self.handle = None
        self.name = name


class _Op:
    __slots__ = ("emit", "waits_c", "waits_d", "signal", "sig_no", "dma")

    def __init__(self, emit, waits_c, waits_d, dma):
        self.emit = emit
        self.waits_c = waits_c
        self.waits_d = waits_d
        self.signal = False
        self.sig_no = 0
        self.dma = dma


class Sched:
    def __init__(self):
        self.streams = {e: [] for e in ALL_ENG}
        self.seen_c = {e: {} for e in ALL_ENG}
        self.seen_d = {e: {} for e in ALL_ENG}
        self.dma_sems = []
        self.plan = False
        self.same_sync = False

    def dma_sem(self, name):
        s = DmaSem(name)
        self.dma_sems.append(s)
        return s

    def op(self, eng, emit, reads=(), writes=(), dma=None):
        if self.plan:
            return None
        deps_c = {}
        deps_d = {}

        def add(tok):
            if tok is None:
                return
            if tok[0] == "c":
                if tok[2] > deps_c.get(tok[1], -1):
                    deps_c[tok[1]] = tok[2]
            else:
                k = id(tok[1])
                if k not in deps_d or tok[2] > deps_d[k][1]:
                    deps_d[k] = (tok[1], tok[2])

        for r in reads:
            add(r.w)
        for w in writes:
            add(w.w)
            for t in w.r.values():
                add(t)
        waits_c = []
        sc = self.seen_c[eng]
        for e2, idx in deps_c.items():
            if e2 == eng and not (SAME_ENGINE_SYNC or self.same_sync):
                continue
            if idx > sc.get(e2, -1):
                sc[e2] = idx
                waits_c.append((e2, idx))
        waits_d = []
        sd = self.seen_d[eng]
        for k, (sem, val) in deps_d.items():
            if val > sd.get(k, 0):
                sd[k] = val
                waits_d.append((sem, val))
        stream = self.streams[eng]
        idx = len(stream)
        if dma is not None:
            dma.count += 16
            tok = ("d", dma, dma.count)
            key = id(dma)
        else:
            tok = ("c", eng, idx)
            key = eng
        for r in reads:
            r.r[key] = tok
        for w in writes:
            w.w = tok
            w.r = {}
        stream.append(_Op(emit, waits_c, waits_d, dma))
        return tok

    def emit(self, nc):
        for eng in ALL_ENG:
            for op in self.streams[eng]:
                for (e2, idx) in op.waits_c:
                    self.streams[e2][idx].signal = True
        for eng in ALL_ENG:
            n = 0
            for op in self.streams[eng]:
                if op.signal:
                    n += 1
                    op.sig_no = n
        with ExitStack() as st:
            csem = {e: st.enter_context(nc.semaphore("s_" + e)) for e in COMPUTE}
            for s in self.dma_sems:
                if s.count > 0:
                    s.handle = st.enter_context(nc.semaphore("d_" + s.name))
            block = st.enter_context(nc.Block())
            streams = self.streams

            def run(eng_name, engine):
                for op in streams[eng_name]:
                    for (e2, idx) in op.waits_c:
                        engine.wait_ge(csem[e2], streams[e2][idx].sig_no)
                    for (sem, val) in op.waits_d:
                        engine.wait_ge(sem.handle, val)
                    ins = op.emit(engine)
                    if op.dma is not None:
                        ins.then_inc(op.dma.handle, 16)
                    elif op.signal:
                        ins.then_inc(csem[eng_name], 1)
                last = {}
                for op in streams[eng_name]:
                    if op.dma is not None:
                        last[id(op.dma)] = op.dma
                for s in last.values():
                    engine.wait_ge(s.handle, s.count)

            @block.tensor
            def _(e):
                run("pe", e)

            @block.scalar
            def _(e):
                run("act", e)

            @block.vector
            def _(e):
                run("dve", e)

            @block.gpsimd
            def _(e):
                run("pool", e)

            @block.sync
            def _(e):
                run("sp", e)


class Slot:
    __slots__ = ("t", "res", "sem")


class WStream:
    def __init__(self, K, name, slots):
        self.K = K
        self.name = name
        self.slots = slots
        self.specs = []
        self.fences = []
        self.reset()

    def reset(self):
        self.next_use = 0
        self.next_release = 0
        self.loaded = 0
        self.fence_passed = 0
        self.extra = []

    def _load(self, i):
        slot = self.slots[i % len(self.slots)]
        for piece in self.specs[i]:
            dst_fn, src = piece
            self.K.s.op("pool", (lambda e, d=dst_fn(slot.t), s_=src: e.dma_start(out=d, in_=s_)),
                        writes=[slot.res] + self.extra, dma=slot.sem)
        self.loaded = i + 1

    def _pump(self):
        limit = self.fences[self.fence_passed] if self.fence_passed < len(self.fences) else len(self.specs)
        hi = min(len(self.specs), self.next_release + len(self.slots), limit)
        while self.loaded < hi:
            self._load(self.loaded)

    def prime(self):
        self._pump()

    def fence(self, extra_res):
        if self.K.s.plan:
            self.fences.append(len(self.specs))
            return
        self.fence_passed += 1
        self.extra = list(extra_res)
        self._pump()
        self.extra = []

    def acquire(self, spec):
        K = self.K
        if K.s.plan:
            self.specs.append(spec)
            return self.slots[0]
        i = self.next_use
        assert i < self.loaded, (self.name, i, self.loaded)
        self.next_use += 1
        return self.slots[i % len(self.slots)]

    def release(self):
        if self.K.s.plan:
            return
        self.next_release += 1
        self._pump()


TT = namedtuple("TT", "c0 n kind")
TT_H = TT(0, HALO, "h")
TT_0 = TT(HALO, 512, 0)
TT_1 = TT(HALO + 512, 512, 1)
TTS3 = (TT_H, TT_0, TT_1)
TTS2 = (TT_0, TT_1)


class Kern:
    pass


def build(stage, stop=None):
    nc = bass.Bass("TRN2", target_bir_lowering=False)
    K = Kern()
    K.nc = nc
    K.s = Sched()
    s = K.s
    doA = stage in ("A", "F")
    doB = stage in ("B", "F")
    doC = stage in ("C", "F")
    fused = stage == "F"

    def din(name, shape, dt=F32):
        return nc.dram_tensor(name, list(shape), dt, kind="ExternalInput").ap()

    _douts = {}

    def dout(name, shape, dt=F32):
        if name not in _douts:
            _douts[name] = nc.dram_tensor(name, list(shape), dt, kind="ExternalOutput").ap()
        return _douts[name]

    lng = din("lng", [128, 6, KC])
    lnb = din("lnb", [128, 6, KC])
    cst = din("cst", [128, 3, 128])
    if doA or doC:
        wgate = din("wgate", [2, 2, D, DFF])
        wup = din("wup", [2, 2, D, DFF])
        wdown = din("wdown", [2, 2, DFF, D])
    if doA:
        xT = din("xT", [D, TH])
        wmixin = din("wmixin", [D, 4096])
        wpool = din("wpool", [4, 256, 256])
        pscale = din("pscale", [128, 8])
        convw = din("convw", [128, 8, 3])
        hmask = din("hmask", [128, 1])
        invc = din("invc", [128, 4, 16])
        wmixout = din("wmixout", [D, D])
    if doB:
        wqkv = din("wqkv", [D, 3, 4, 128])
        band = din("band", [128, 896])
    if doC:
        wattout = din("wattout", [D, D])
    if fused:
        agin = nc.dram_tensor("agin", [D, T], BF16).ap()
        xg = nc.dram_tensor("xgath", [4 * D, T], BF16).ap()
        a2in = nc.dram_tensor("a2in", [4 * 512, T], BF16).ap()
        a2out = nc.dram_tensor("a2out", [4 * 512, T], BF16).ap()
        outT = dout("outT", [D, T])
    else:
        if stage == "A":
            xf_out = dout("xf_out", [D, T])
            xb_out = dout("xb_out", [D, T], BF16)
        if stage == "B":
            xg = din("xgath", [4 * D, T], BF16)
            oT = dout("oT", [512, SEQ], BF16)
        if stage == "C":
            xf_in = din("xf_in", [D, T])
            a2out = din("om", [D, T], BF16)
            outT = dout("outT", [D, T])

    base0 = (nc.sbuf_base + 31) // 32 * 32
    arena = nc.alloc_sbuf_tensor("arena", [128, (nc.sbuf_top - base0) // 32 * 32], mybir.dt.uint8)
    off = [base0]
    sb_limit = base0 + (nc.sbuf_top - base0) // 32 * 32

    def at(name, shape, dt, o=None, advance=True):
        nbytes = int(np.prod(shape[1:])) * (4 if dt == F32 else 2)
        if o is None:
            o = off[0]
            if advance:
                off[0] += (nbytes + 31) // 32 * 32
        return nc.alloc_sbuf_tensor_at(name, list(shape), dt, offset=o)

    XF = at("XF", [128, KC, TH], F32)
    o_xb = off[0]
    XB = at("XB", [128, KC, TH], BF16)
    o_w = off[0]
    wslots = []
    for i in range(4):
        sl = Slot()
        sl.t = at("W%d" % i, [128, KC, 256], BF16)
        sl.res = Res()
        sl.sem = s.dma_sem("w%d" % i)
        wslots.append(sl)
    o_rs = off[0]
    RS_SIZE = 49152
    off[0] += RS_SIZE
    AT = at("AT", [128, GC, TH], BF16, o=o_rs)
    wdslots = []
    for i in range(2):
        sl = Slot()
        sl.t = at("WD%d" % i, [128, GC, 512], BF16, o=o_rs + GC * TH * 2 + i * GC * 512 * 2)
        sl.res = Res()
        sl.sem = s.dma_sem("wd%d" % i)
        wdslots.append(sl)
    assert GC * TH * 2 + 2 * GC * 512 * 2 <= RS_SIZE
    YM = at("YM", [128, KC, T], BF16, o=o_rs)
    MT = [at("MT%d" % i, [128, TH], F32, o=o_rs + KC * T * 2 + i * TH * 4) for i in range(3)]
    assert KC * T * 2 + 3 * TH * 4 <= RS_SIZE
    QT = [at("QT%d" % i, [128, SEQ], BF16, o=o_rs + i * 24576) for i in range(2)]
    KT = [at("KT%d" % i, [128, SEQ], BF16, o=o_rs + i * 24576 + 8192) for i in range(2)]
    VV = [at("VV%d" % i, [128, 32, 128], BF16, o=o_rs + i * 24576 + 16384) for i in range(2)]
    XG = [at("XG%d" % i, [128, KC, 512], BF16, o=o_xb + i * 16384) for i in range(2)]
    o_tmp = off[0]
    off[0] += 8 * 2048
    TF = [at("TF%d" % i, [128, 512], F32, o=o_tmp + i * 2048) for i in range(8)]
    TB = [at("TB%d" % i, [128, 1024], BF16, o=o_tmp + i * 2048) for i in range(8)]
    PW = at("PW", [128, 4, 2, 256], BF16, o=o_tmp + 4 * 2048)
    rT_ = [Res() for _ in range(8)]
    rTh = [[Res(), Res()] for _ in range(8)]

    def RW(i):
        return (rT_[i], rTh[i][0], rTh[i][1])
    CST = at("CST", [128, 3, 128], BF16)
    LNG = at("LNG", [128, 6, KC], F32)
    LNB = at("LNB", [128, 6, KC], F32)
    LNAB = at("LNAB", [128, 6, KC], F32)
    EPS = at("EPS", [128, 2], F32)
    if doA:
        PSC = at("PSC", [128, 8], F32)
        CVW = at("CVW", [128, 8, 3], F32)
        HM = at("HM", [128, 1], F32)
        INVC = at("INVC", [128, 4, 16], F32)
    if doB:
        BAND = at("BAND", [128, 896], BF16)
    assert off[0] <= sb_limit, (off[0], sb_limit)
    ONES = CST[:, 0, :]
    NTRI = CST[:, 1, :]
    NONES = CST[:, 2, :]

    pb = [nc.alloc_psum_tensor("pb%d" % i, [128, 512], F32) for i in range(8)]
    rpb = [Res() for _ in range(8)]
    rhalo = [Res() for _ in range(32)]
    halo_ctr = [0]

    def halo_slot():
        i = halo_ctr[0] % 32
        halo_ctr[0] += 1
        return pb[4][:, 16 * i:16 * i + 16], rhalo[i]

    rXF = [[Res() for _ in range(3)] for _ in range(KC)]
    rXB = [[Res() for _ in range(3)] for _ in range(KC)]
    rAT = [[Res() for _ in range(3)] for _ in range(GC)]
    rYM = [[Res() for _ in range(2)] for _ in range(KC)]
    rMT = [Res() for _ in range(3)]
    rC = Res()
    rPW = Res()

    def ti(tt):
        return 0 if tt.kind == "h" else 1 + tt.kind

    def cols(tt):
        return slice(tt.c0, tt.c0 + tt.n)

    W = WStream(K, "W", wslots)
    WD = WStream(K, "WD", wdslots)

    def wspec_cols(src2d, c0, n, dst0=0):
        return ((lambda t, a=dst0, b=n: t[:, :, a:a + b]),
                src2d[:, c0:c0 + n].rearrange("(c p) f -> p c f", p=128))

    def mm_group(e, out_ap, pairs):
        n = len(pairs)
        ins = None
        for i, (l, r) in enumerate(pairs):
            ins = e.matmul(out_ap, lhsT=l, rhs=r, start=(i == 0), stop=(i == n - 1))
        return ins

    dsem_misc = [s.dma_sem("m%d" % i) for i in range(8)]
    misc_ctr = [0]

    def misc_sem():
        i = misc_ctr[0] % 8
        misc_ctr[0] += 1
        return dsem_misc[i]

    def setup():
        s.op("pool", lambda e: e.dma_start(out=CST[:], in_=cst), writes=[rC], dma=misc_sem())
        s.op("sp", lambda e: e.dma_start(out=LNG[:], in_=lng), writes=[rC], dma=misc_sem())
        s.op("sp", lambda e: e.dma_start(out=LNB[:], in_=lnb), writes=[rC], dma=misc_sem())
        s.op("dve", lambda e: e.memset(EPS[:, 0:1], LN_EPS), writes=[rC])
        s.op("dve", lambda e: e.memset(EPS[:, 1:2], 1.0), writes=[rC])
        s.op("dve", lambda e: e.tensor_scalar(out=LNAB[:], in0=LNB[:], scalar1=ALPHA, scalar2=None, op0=ALU.mult),
             reads=[rC], writes=[rC])
        if doA:
            s.op("sp", lambda e: e.dma_start(out=PSC[:], in_=pscale), writes=[rC], dma=misc_sem())
            s.op("sp", lambda e: e.dma_start(out=CVW[:], in_=convw), writes=[rC], dma=misc_sem())
            s.op("sp", lambda e: e.dma_start(out=HM[:], in_=hmask), writes=[rC], dma=misc_sem())
            s.op("sp", lambda e: e.dma_start(out=INVC[:], in_=invc), writes=[rC], dma=misc_sem())
        if doB:
            s.op("pool", lambda e: e.dma_start(out=BAND[:], in_=band), writes=[rC], dma=misc_sem())

    sctr = [0]

    def ffn(l, j, tts):
        wg = wgate[l, j]
        wu = wup[l, j]
        wd = wdown[l, j]
        cur = {}
        f2ctr = [0]
        for g in range(NG):
            for fl in range(GC):
                f = g * GC + fl
                bi, half = divmod(f, 2)
                if half == 0 or not cur:
                    cur["g"] = W.acquire([wspec_cols(wg, bi * 256, 256)])
                    cur["u"] = W.acquire([wspec_cols(wu, bi * 256, 256)])
                sg, su = cur["g"], cur["u"]
                cs = slice(half * 128, half * 128 + 128)
                for tt in tts:
                    if tt.kind == "h":
                        ph, rh = halo_slot()
                        pu, ru = halo_slot()
                    else:
                        ph, rh = pb[tt.kind][:], rpb[tt.kind]
                        pu, ru = pb[2 + tt.kind][:], rpb[2 + tt.kind]
                    xr = [rXB[c][ti(tt)] for c in range(KC)]

                    def mm(e, ph=ph, pu=pu, sg=sg, su=su, cs=cs, tt=tt):
                        mm_group(e, ph, [(sg.t[:, c, cs], XB[:, c, cols(tt)]) for c in range(KC)])
                        return mm_group(e, pu, [(su.t[:, c, cs], XB[:, c, cols(tt)]) for c in range(KC)])
                    s.op("pe", mm, reads=xr + [sg.res, su.res], writes=[rh, ru])
                    si = sctr[0] % 2
                    sctr[0] += 1
                    S = TF[si][:, 0:tt.n]
                    s.op("act", lambda e, S=S, ph=ph: e.activation(out=S, in_=ph, func=AF.Silu),
                         reads=[rh], writes=[*RW(si)])
                    s.op("dve", lambda e, S=S, pu=pu, fl=fl, tt=tt: e.tensor_tensor(
                        out=AT[:, fl, cols(tt)], in0=S, in1=pu, op=ALU.mult),
                        reads=[*RW(si), ru], writes=[rAT[fl][ti(tt)]])
                if half == 1 or f == FC - 1:
                    W.release()
                    W.release()
                    cur.clear()
            for db in range(4):
                sd = WD.acquire([((lambda t: t[:]),
                                  wd[g * GC * 128:(g + 1) * GC * 128, db * 512:(db + 1) * 512].rearrange(
                                      "(f p) d -> p f d", p=128))])
                for dl in range(4):
                    dc = db * 4 + dl
                    for tt in tts:
                        if tt.kind == "h":
                            pf, rf = halo_slot()
                        else:
                            bk = f2ctr[0] % 4
                            f2ctr[0] += 1
                            pf, rf = pb[bk][:], rpb[bk]
                        ar = [rAT[fl][ti(tt)] for fl in range(GC)]
                        s.op("pe", lambda e, pf=pf, sd=sd, dl=dl, tt=tt: mm_group(
                            e, pf, [(sd.t[:, fl, dl * 128:(dl + 1) * 128], AT[:, fl, cols(tt)]) for fl in range(GC)]),
                            reads=ar + [sd.res], writes=[rf])
                        s.op("dve", lambda e, pf=pf, dc=dc, tt=tt: e.scalar_tensor_tensor(
                            out=XF[:, dc, cols(tt)], in0=pf, scalar=0.5, in1=XF[:, dc, cols(tt)],
                            op0=ALU.mult, op1=ALU.add),
                            reads=[rf, rXF[dc][ti(tt)]], writes=[rXF[dc][ti(tt)]])
                WD.release()

    def layer_norm(li, tts, final=False):
        for tt in tts:
            layer_norm_tt(li, tt, final)

    def layer_norm_tt(li, tt, final):
        if True:
            n = tt.n
            psum_, rsum = pb[5][:, 0:n], rpb[5]
            psq_, rsq = pb[6][:, 0:n], rpb[6]
            tcol = cols(tt)
            for c in range(KC):
                hb = c % 2
                yb = TB[2][:, hb * 512:hb * 512 + n]
                sq = TB[3][:, hb * 512:hb * 512 + n]
                s.op("dve", lambda e, yb=yb, c=c: e.tensor_copy(out=yb, in_=XF[:, c, tcol]),
                     reads=[rXF[c][ti(tt)]], writes=[rTh[2][hb]])
                s.op("act", lambda e, sq=sq, c=c: e.activation(out=sq, in_=XF[:, c, tcol], func=AF.Square),
                     reads=[rXF[c][ti(tt)]], writes=[rTh[3][hb]])

                def mm(e, yb=yb, sq=sq, c=c):
                    e.matmul(psum_, lhsT=ONES, rhs=yb, start=(c == 0), stop=(c == KC - 1))
                    return e.matmul(psq_, lhsT=ONES, rhs=sq, start=(c == 0), stop=(c == KC - 1))
                wr = [rsum] if rsum is rsq else [rsum, rsq]
                s.op("pe", mm, reads=[rTh[2][hb], rTh[3][hb], rC], writes=wr)
            M = TF[4][:, 0:n]
            V = TF[1][:, 0:n]
            R = TF[5][:, 0:n]
            NM = TF[6][:, 0:n]
            s.op("dve", lambda e: e.tensor_scalar(out=M, in0=psum_, scalar1=1.0 / D, scalar2=None, op0=ALU.mult),
                 reads=[rsum], writes=[*RW(4)])
            s.op("dve", lambda e: e.tensor_tensor(out=V, in0=M, in1=M, op=ALU.mult), reads=[*RW(4)], writes=[*RW(1)])
            s.op("dve", lambda e: e.scalar_tensor_tensor(out=V, in0=psq_, scalar=1.0 / D, in1=V,
                                                         op0=ALU.mult, op1=ALU.subtract),
                 reads=[rsq, *RW(1)], writes=[*RW(1)])
            s.op("act", lambda e: e.activation(out=V, in_=V, func=AF.Ln, bias=EPS[:, 0:1], scale=1.0),
                 reads=[*RW(1), rC], writes=[*RW(1)])
            s.op("act", lambda e: e.activation(out=R, in_=V, func=AF.Exp, scale=-0.5), reads=[*RW(1)], writes=[*RW(5)])
            s.op("dve", lambda e: e.scalar_tensor_tensor(out=NM, in0=M, scalar=-1.0, in1=R, op0=ALU.mult, op1=ALU.mult),
                 reads=[*RW(4), *RW(5)], writes=[*RW(6)])
            for c in range(KC):
                tsl = 7 if c % 2 == 0 else 0
                t1 = TF[tsl][:, 0:n]
                gc = LNG[:, li, c:c + 1]
                s.op("dve", lambda e, t1=t1, c=c, gc=gc: e.scalar_tensor_tensor(
                    out=t1, in0=XF[:, c, tcol], scalar=gc, in1=R, op0=ALU.mult, op1=ALU.mult),
                    reads=[rXF[c][ti(tt)], *RW(5), rC], writes=[*RW(tsl)])
                s.op("dve", lambda e, t1=t1, gc=gc: e.scalar_tensor_tensor(
                    out=t1, in0=NM, scalar=gc, in1=t1, op0=ALU.mult, op1=ALU.add),
                    reads=[*RW(6), *RW(tsl), rC], writes=[*RW(tsl)])
                if final:
                    s.op("act", lambda e, t1=t1, c=c: e.activation(
                        out=XF[:, c, tcol], in_=t1, func=AF.Identity, bias=LNB[:, li, c:c + 1], scale=1.0),
                        reads=[*RW(tsl), rC], writes=[rXF[c][ti(tt)]])
                else:
                    s.op("act", lambda e, t1=t1, c=c: e.activation(
                        out=XB[:, c, tcol], in_=t1, func=AF.Identity, bias=LNB[:, li, c:c + 1], scale=1.0),
                        reads=[*RW(tsl), rC], writes=[rXB[c][ti(tt)]])
                    s.op("act", lambda e, t1=t1, c=c: e.activation(
                        out=XF[:, c, tcol], in_=t1, func=AF.Identity, bias=LNAB[:, li, c:c + 1], scale=ALPHA),
                        reads=[*RW(tsl), rC], writes=[rXF[c][ti(tt)]])

    def proj_chunk(slot, cs, tts, want_halo=True):
        outs = []
        pair = proj_ctr[0] % 2
        proj_ctr[0] += 1
        for tt in tts:
            if tt.kind == "h":
                p, r = halo_slot()
            else:
                b = pair * 2 + tt.kind
                p, r = pb[b][:], rpb[b]
            xr = [rXB[c][ti(tt)] for c in range(KC)]
            s.op("pe", lambda e, p=p, slot=slot, cs=cs, tt=tt: mm_group(
                e, p, [(slot.t[:, c, cs], XB[:, c, cols(tt)]) for c in range(KC)]),
                reads=xr + [slot.res], writes=[r])
            outs.append((tt, p, r))
        return outs

    proj_ctr = [0]

    def mixer0():
        mw = wmixin
        s.op("pool", lambda e: e.dma_start(out=PW[:], in_=wpool.rearrange("g (i p) c -> p g i c", p=128)),
             reads=[], writes=[rPW, *RW(4), *RW(5)], dma=misc_sem())
        U, A_, B_ = MT[0], MT[1], MT[2]
        rU, rA, rB = rMT
        for g in range(4):
            win = 2 ** (g + 1)
            slot = W.acquire([wspec_cols(mw, 256 * g, 256)])
            for i in range(2):
                ch = 2 * g + i
                outs = proj_chunk(slot, slice(i * 128, i * 128 + 128), TTS3)
                for (tt, p, r) in outs:
                    if tt.kind == "h":
                        s.op("act", lambda e, p=p: e.activation(out=U[:, 0:16], in_=p, func=AF.Identity,
                                                                scale=HM[:, 0:1]),
                             reads=[r, rC], writes=[rU])
                    else:
                        s.op("act", lambda e, p=p, tt=tt: e.activation(out=U[:, cols(tt)], in_=p, func=AF.Identity),
                             reads=[r], writes=[rU])
                src, rsrc = U, rU
                st = 1
                pp = [(A_, rA), (B_, rB)]
                k = 0
                while st < win:
                    dst, rdst = pp[k % 2]
                    lo = 2 * st - 1
                    s.op("dve", lambda e, dst=dst, src=src, lo=lo, st=st: e.tensor_tensor(
                        out=dst[:, lo:TH], in0=src[:, lo:TH], in1=src[:, lo - st:TH - st], op=ALU.add),
                        reads=[rsrc], writes=[rdst])
                    src, rsrc = dst, rdst
                    st *= 2
                    k += 1
                DM = TB[2 + i]
                s.op("dve", lambda e, src=src, DM=DM, win=win: e.scalar_tensor_tensor(
                    out=DM[:, 0:T], in0=src[:, HALO:TH], scalar=1.0 / win, in1=U[:, HALO:TH],
                    op0=ALU.mult, op1=ALU.subtract),
                    reads=[rsrc, rU], writes=[*RW(2 + i)])
                oth, roth = pp[k % 2]
                s.op("dve", lambda e, src=src, oth=oth, g=g: e.tensor_tensor(
                    out=oth[:, 0:16], in0=src[:, HALO:HALO + 16], in1=INVC[:, g, :], op=ALU.mult),
                    reads=[rsrc, rC], writes=[roth])
                s.op("dve", lambda e, oth=oth, DM=DM: e.tensor_tensor(
                    out=DM[:, 0:16], in0=oth[:, 0:16], in1=U[:, HALO:HALO + 16], op=ALU.subtract),
                    reads=[roth, rU], writes=[*RW(2 + i)])
            W.release()
            for jj in range(2):
                ch = 2 * g + jj
                for tt in TTS2:
                    b = 5 + tt.kind
                    s.op("pe", lambda e, b=b, g=g, jj=jj, tt=tt: mm_group(
                        e, pb[b][:], [(PW[:, g, i2, jj * 128:(jj + 1) * 128],
                                       TB[2 + i2][:, tt.c0 - HALO:tt.c0 - HALO + 512]) for i2 in range(2)]),
                        reads=[rPW, *RW(2), *RW(3)], writes=[rpb[b]])
                    s.op("act", lambda e, b=b, ch=ch, tt=tt: e.activation(
                        out=YM[:, ch, tt.c0 - HALO:tt.c0 - HALO + 512], in_=pb[b][:], func=AF.Identity,
                        scale=PSC[:, ch:ch + 1]),
                        reads=[rpb[b], rC], writes=[rYM[ch][tt.kind]])
        for ip in range(4):
            slC = W.acquire([wspec_cols(mw, 2048 + 256 * ip, 256)])
            slX = W.acquire([wspec_cols(mw, 3072 + 256 * ip, 256)])
            slB = W.acquire([wspec_cols(mw, 1024 + 256 * ip, 256)])
            for i in range(2):
                ci = 2 * ip + i
                cs = slice(i * 128, i * 128 + 128)
                Csb, rCs = MT[0], rMT[0]
                Z, rZ = MT[1], rMT[1]
                ACC, rACC = MT[2], rMT[2]
                for (tt, p, r) in proj_chunk(slC, cs, TTS3):
                    if tt.kind == "h":
                        s.op("act", lambda e, p=p: e.activation(out=Csb[:, 0:16], in_=p, func=AF.Identity,
                                                                scale=HM[:, 0:1]),
                             reads=[r, rC], writes=[rCs])
                    else:
                        s.op("act", lambda e, p=p, tt=tt: e.activation(out=Csb[:, cols(tt)], in_=p, func=AF.Identity),
                             reads=[r], writes=[rCs])
                for (tt, p, r) in proj_chunk(slX, cs, TTS3):
                    s.op("dve", lambda e, p=p, tt=tt: e.tensor_tensor(
                        out=Z[:, cols(tt)], in0=Csb[:, cols(tt)], in1=p, op=ALU.mult),
                        reads=[r, rCs], writes=[rZ])
                s.op("dve", lambda e, ci=ci: e.tensor_scalar(
                    out=ACC[:, 0:T], in0=Z[:, HALO:TH], scalar1=CVW[:, ci, 2:3], scalar2=None, op0=ALU.mult),
                    reads=[rZ, rC], writes=[rACC])
                s.op("dve", lambda e, ci=ci: e.scalar_tensor_tensor(
                    out=ACC[:, 0:T], in0=Z[:, HALO - 1:TH - 1], scalar=CVW[:, ci, 1:2], in1=ACC[:, 0:T],
                    op0=ALU.mult, op1=ALU.add),
                    reads=[rZ, rC, rACC], writes=[rACC])
                s.op("dve", lambda e, ci=ci: e.scalar_tensor_tensor(
                    out=ACC[:, 0:T], in0=Z[:, HALO - 2:TH - 2], scalar=CVW[:, ci, 0:1], in1=ACC[:, 0:T],
                    op0=ALU.mult, op1=ALU.add),
                    reads=[rZ, rC, rACC], writes=[rACC])
                for (tt, p, r) in proj_chunk(slB, cs, TTS2):
                    s.op("dve", lambda e, p=p, tt=tt, ci=ci: e.tensor_tensor(
                        out=YM[:, 8 + ci, tt.c0 - HALO:tt.c0 - HALO + 512],
                        in0=ACC[:, tt.c0 - HALO:tt.c0 - HALO + 512], in1=p, op=ALU.mult),
                        reads=[r, rACC], writes=[rYM[8 + ci][tt.kind]])
            W.release()
            W.release()
            W.release()
        out_proj(wmixout, YM, rYM)

    def out_proj(wsrc, Y, rY):
        for ob in range(8):
            slot = W.acquire([wspec_cols(wsrc, 256 * ob, 256)])
            for i in range(2):
                dc = 2 * ob + i
                pair = proj_ctr[0] % 2
                proj_ctr[0] += 1
                for tt in TTS2:
                    b = pair * 2 + tt.kind
                    s.op("pe", lambda e, b=b, slot=slot, i=i, tt=tt: mm_group(
                        e, pb[b][:], [(slot.t[:, c, i * 128:(i + 1) * 128],
                                       Y[:, c, tt.c0 - HALO:tt.c0 - HALO + 512]) for c in range(KC)]),
                        reads=[rY[c][tt.kind] for c in range(KC)] + [slot.res], writes=[rpb[b]])
                    s.op("dve", lambda e, b=b, dc=dc, tt=tt: e.tensor_tensor(
                        out=XF[:, dc, cols(tt)], in0=pb[b][:], in1=XF[:, dc, cols(tt)], op=ALU.add),
                        reads=[rpb[b], rXF[dc][ti(tt)]], writes=[rXF[dc][ti(tt)]])
            W.release()

    rXG = [Res(), Res()]
    xg_sem = [s.dma_sem("xg0"), s.dma_sem("xg1")]
    rQ = [[Res() for _ in range(8)] for _ in range(2)]
    rK = [[Res() for _ in range(8)] for _ in range(2)]
    rV = [[Res() for _ in range(8)] for _ in range(2)]
    xg_ctr = [0]
    rXGdram = Res()
    rOdram = Res()
    osem = [s.dma_sem("o0"), s.dma_sem("o1")]

    def proj_tile(hl, j, slQK, slV):
        buf = hl % 2
        xi = xg_ctr[0] % 2
        xg_ctr[0] += 1
        xs = XG[xi]
        src = xg[(j // 2) * D:(j // 2 + 1) * D, (j % 2) * 512:(j % 2) * 512 + 512].rearrange("(c p) t -> p c t", p=128)
        s.op("sp", lambda e, xs=xs, src=src: e.dma_start(out=xs[:], in_=src),
             reads=[rXGdram], writes=[rXG[xi]] + ([rXB[c][k] for c in range(KC) for k in range(3)] if xg_ctr[0] <= 2 else []),
             dma=xg_sem[xi])
        s.op("pe", lambda e, xs=xs: mm_group(e, pb[6][:], [(slQK.t[:, c, 0:128], xs[:, c, :]) for c in range(KC)]),
             reads=[rXG[xi], slQK.res], writes=[rpb[6]])
        s.op("act", lambda e: e.activation(out=QT[buf][:, j * 512:(j + 1) * 512], in_=pb[6][:], func=AF.Identity,
                                           scale=QSCALE),
             reads=[rpb[6]], writes=[rQ[buf][j]])
        s.op("pe", lambda e, xs=xs: mm_group(e, pb[7][:], [(slQK.t[:, c, 128:256], xs[:, c, :]) for c in range(KC)]),
             reads=[rXG[xi], slQK.res], writes=[rpb[7]])
        s.op("dve", lambda e: e.tensor_copy(out=KT[buf][:, j * 512:(j + 1) * 512], in_=pb[7][:]),
             reads=[rpb[7]], writes=[rK[buf][j]])

        def mmv(e, xs=xs):
            ins = None
            for blk in range(4):
                ins = mm_group(e, pb[6][:, blk * 128:(blk + 1) * 128],
                               [(xs[:, c, blk * 128:(blk + 1) * 128], slV.t[:, c, 0:128]) for c in range(KC)])
            return ins
        s.op("pe", mmv, reads=[rXG[xi], slV.res], writes=[rpb[6]])
        s.op("dve", lambda e: e.tensor_copy(out=VV[buf][:, 4 * j:4 * j + 4, :],
                                            in_=pb[6][:].rearrange("p (b d) -> p b d", b=4)),
             reads=[rpb[6]], writes=[rV[buf][j]])

    SPB = [(TB[2], 0, rTh[2][0]), (TB[2], 1, rTh[2][1]), (TB[6], 0, rTh[6][0]), (TB[6], 1, rTh[6][1])]
    ABF = [(TB[4], 0, rTh[4][0]), (TB[4], 1, rTh[4][1]), (TB[7], 0, rTh[7][0]), (TB[7], 1, rTh[7][1])]

    def half(t, h):
        return t[:, h * 512:h * 512 + 512]

    def attention():
        def get_w(hl):
            a = W.acquire([((lambda t: t[:, :, 0:128]), wqkv[:, 0, hl, :].rearrange("(c p) f -> p c f", p=128)),
                           ((lambda t: t[:, :, 128:256]), wqkv[:, 1, hl, :].rearrange("(c p) f -> p c f", p=128))])
            b = W.acquire([((lambda t: t[:, :, 0:128]), wqkv[:, 2, hl, :].rearrange("(c p) f -> p c f", p=128))])
            return a, b
        pairs = []
        for hl in range(4):
            for jq in range(8):
                nkb = 4 * jq + 4
                for kb in range(nkb - 1, -1, -1):
                    pairs.append((hl, jq, kb, kb == nkb - 1, kb == 0))
        N = len(pairs)
        cbi = {}
        wsl = {}

        def info(n):
            hl, jq, kb, first, last = pairs[n]
            buf = hl % 2
            diag = kb >= 4 * jq
            dd = kb - 4 * jq
            mask = BAND[:, 384 - 128 * dd:384 - 128 * dd + 512] if diag else None
            return hl, jq, kb, first, last, buf, mask

        def st_z(n):
            hl, jq, kb, first, last, buf, mask = info(n)
            zb = n % 2
            ks = KT[buf][:, kb * 128:(kb + 1) * 128]
            qs = QT[buf][:, jq * 512:(jq + 1) * 512]
            spt, sph, rSP = SPB[n % 4]
            SP = half(spt, sph)
            E = TF[zb]
            s.op("pe", lambda e: e.matmul(pb[zb][:], lhsT=ks, rhs=qs, start=True, stop=True),
                 reads=[rK[buf][kb // 4], rQ[buf][jq]], writes=[rpb[zb]])
            s.op("act", lambda e: e.activation(out=E[:], in_=pb[zb][:], func=AF.Exp),
                 reads=[rpb[zb]], writes=[*RW(zb)])
            s.op("act", lambda e: e.activation(out=SP, in_=E[:], func=AF.Ln, bias=EPS[:, 1:2], scale=1.0),
                 reads=[*RW(zb), rC], writes=[rSP])
            if mask is not None:
                s.op("dve", lambda e: e.tensor_tensor(out=SP, in0=SP, in1=mask, op=ALU.mult),
                     reads=[rSP, rC], writes=[rSP])

        def st_la(n):
            hl, jq, kb, first, last, buf, mask = info(n)
            lb = 2 + n % 2
            ks = KT[buf][:, kb * 128:(kb + 1) * 128]
            qs = QT[buf][:, jq * 512:(jq + 1) * 512]
            spt, sph, rSP = SPB[n % 4]
            SP = half(spt, sph)
            at_, ah, rA = ABF[n % 4]
            A = half(at_, ah)
            cb = None if first else cbi[n]

            def mml(e):
                e.matmul(pb[lb][:], lhsT=ks, rhs=qs, start=True, stop=False)
                ins = e.matmul(pb[lb][:], lhsT=NTRI, rhs=SP, start=False, stop=first)
                if not first:
                    ins = e.matmul(pb[lb][:], lhsT=NONES, rhs=half(TB[3], cb), start=False, stop=True)
                return ins
            rd = [rK[buf][kb // 4], rQ[buf][jq], rSP, rC]
            if not first:
                rd.append(rTh[3][cb])
            s.op("pe", mml, reads=rd, writes=[rpb[lb]])
            if not last:
                if first:
                    s.op("dve", lambda e: e.tensor_copy(out=half(TB[3], 0), in_=SP), reads=[rSP], writes=[rTh[3][0]])
                    cbi[n + 1] = 0
                else:
                    nx = 1 - cb
                    s.op("dve", lambda e: e.tensor_tensor(out=half(TB[3], nx), in0=half(TB[3], cb), in1=SP, op=ALU.add),
                         reads=[rSP, rTh[3][cb]], writes=[rTh[3][nx]])
                    cbi[n + 1] = nx
            s.op("act", lambda e: e.activation(out=A, in_=pb[lb][:], func=AF.Exp), reads=[rpb[lb]], writes=[rA])
            if mask is not None:
                s.op("dve", lambda e: e.tensor_tensor(out=A, in0=A, in1=mask, op=ALU.mult),
                     reads=[rA, rC], writes=[rA])

        def st_av(n):
            hl, jq, kb, first, last, buf, mask = info(n)
            ob = 4 + (jq % 2)
            at_, ah, rA = ABF[n % 4]
            A = half(at_, ah)
            s.op("pe", lambda e: e.matmul(pb[ob][:], lhsT=VV[buf][:, kb, :], rhs=A, start=first, stop=last),
                 reads=[rV[buf][kb // 4], rA], writes=[rpb[ob]])
            if last:
                osl = jq % 2
                OS = half(TB[5], osl)
                s.op("dve", lambda e: e.tensor_copy(out=OS, in_=pb[ob][:]), reads=[rpb[ob]], writes=[rTh[5][osl]])
                if fused:
                    r_ = jq // 2
                    dst = a2in[r_ * 512 + hl * 128:r_ * 512 + (hl + 1) * 128, (jq % 2) * 512:(jq % 2) * 512 + 512]
                else:
                    dst = oT[hl * 128:(hl + 1) * 128, jq * 512:(jq + 1) * 512]
                s.op("sp", lambda e: e.dma_start(out=dst, in_=OS), reads=[rTh[5][osl]], writes=[rOdram], dma=osem[osl])

        a, b = get_w(0)
        for j in range(8):
            proj_tile(0, j, a, b)
        W.release()
        W.release()
        for k in range(-2, N):
            if 0 <= k + 2 < N:
                st_z(k + 2)
            if 0 <= k + 1 < N:
                st_la(k + 1)
            if 0 <= k < N:
                hl, jq, kb, first, last = pairs[k]
                st_av(k)
                if first and hl + 1 < 4:
                    if jq == 0:
                        wsl[hl + 1] = get_w(hl + 1)
                    proj_tile(hl + 1, jq, *wsl[hl + 1])
                    if jq == 7:
                        W.release()
                        W.release()

    cc_sem = [s.dma_sem("cc0"), s.dma_sem("cc1")]
    GROUPS = [[0, 1, 2, 3], [4, 5, 6, 7]]

    def program():
        setup()
        allxf = [rXF[c][k] for c in range(KC) for k in range(3)]
        allxb = [rXB[c][k] for c in range(KC) for k in range(3)]
        if doA:
            s.op("sp", lambda e: e.dma_start(out=XF[:], in_=xT.rearrange("(c p) t -> p c t", p=128)),
                 writes=allxf, dma=misc_sem())
            for c in range(KC):
                s.op("dve", lambda e, c=c: e.tensor_copy(out=XB[:, c, :], in_=XF[:, c, :]),
                     reads=rXF[c], writes=rXB[c])
                s.op("act", lambda e, c=c: e.activation(out=XF[:, c, :], in_=XF[:, c, :], func=AF.Identity, scale=ALPHA),
                     reads=rXF[c] + rXB[c], writes=rXF[c])
            def dbg():
                dx = dout("dbg_xf", [D, TH])
                db = dout("dbg_xb", [D, TH], BF16)
                s.op("sp", lambda e: e.dma_start(out=dx.rearrange("(c p) t -> p c t", p=128), in_=XF[:]),
                     reads=allxf, dma=misc_sem())
                s.op("sp", lambda e: e.dma_start(out=db.rearrange("(c p) t -> p c t", p=128), in_=XB[:]),
                     reads=allxb, dma=misc_sem())
            if stop == "init":
                return dbg()
            ffn(0, 0, TTS3)
            if stop == "ffn":
                return dbg()
            layer_norm(0, TTS3)
            if stop == "ln0":
                return dbg()
            s.same_sync = True
            mixer0()
            s.same_sync = False
            if stop == "mixer":
                return dbg()
            def dump(name):
                if stop != "all":
                    return
                dx = dout("dump_" + name, [D, TH])
                s.op("sp", lambda e: e.dma_start(out=dx.rearrange("(c p) t -> p c t", p=128), in_=XF[:]),
                     reads=allxf, dma=misc_sem())
            WD.fence([r for c in range(KC) for r in rYM[c]] + rMT)
            layer_norm(1, TTS2)
            dump("ln1")
            ffn(0, 1, TTS2)
            dump("ffn01")
            layer_norm(2, TTS2)
            dump("ln2")
            ffn(1, 0, TTS2)
            dump("ffn10")
            layer_norm(3, TTS2)
            if fused:
                s.op("sp", lambda e: e.dma_start(out=agin.rearrange("(c p) t -> p c t", p=128), in_=XB[:, :, HALO:TH]),
                     reads=allxb, writes=[rXGdram], dma=misc_sem())
                s.op("pool", lambda e: e.collective_compute("AllGather", ALU.bypass, replica_groups=GROUPS,
                                                            ins=[agin], outs=[xg]),
                     reads=[rXGdram], writes=[rXGdram], dma=cc_sem[0])
            else:
                s.op("sp", lambda e: e.dma_start(out=xb_out.rearrange("(c p) t -> p c t", p=128), in_=XB[:, :, HALO:TH]),
                     reads=allxb, dma=misc_sem())
                s.op("sp", lambda e: e.dma_start(out=xf_out.rearrange("(c p) t -> p c t", p=128), in_=XF[:, :, HALO:TH]),
                     reads=allxf, dma=misc_sem())
        if doB:
            attention()
        if doC:
            OM = XB[:, :, HALO:TH]
            rOM = [[rXB[c][1], rXB[c][2]] for c in range(KC)]
            if fused:
                s.op("pool", lambda e: e.collective_compute("AllToAll", ALU.bypass, replica_groups=GROUPS,
                                                            ins=[a2in], outs=[a2out]),
                     reads=[rOdram], writes=[rOdram], dma=cc_sem[1])
            else:
                s.op("sp", lambda e: e.dma_start(out=XF[:, :, HALO:TH], in_=xf_in.rearrange("(c p) t -> p c t", p=128)),
                     writes=allxf, dma=misc_sem())
            s.op("sp", lambda e: e.dma_start(out=OM, in_=a2out.rearrange("(c p) t -> p c t", p=128)),
                 reads=[rOdram], writes=allxb + rXG, dma=misc_sem())

            class _Y:
                def __getitem__(self, idx):
                    p, c, ts = idx
                    return XB[:, c, HALO + ts.start:HALO + ts.stop]
            WD.fence([r for b_ in range(2) for j_ in range(8) for r in (rQ[b_][j_], rK[b_][j_], rV[b_][j_])])
            out_proj(wattout, _Y(), rOM)
            layer_norm(4, TTS2)
            ffn(1, 1, TTS2)
            layer_norm(5, TTS2, final=True)
            for tt in TTS2:
                s.op("sp", lambda e, tt=tt: e.dma_start(
                    out=outT.rearrange("(c p) t -> p c t", p=128)[:, :, tt.c0 - HALO:tt.c0 - HALO + 512],
                    in_=XF[:, :, cols(tt)]),
                    reads=[rXF[c][ti(tt)] for c in range(KC)], dma=misc_sem())

    s.plan = True
    program()
    s.plan = False
    W.reset()
    WD.reset()
    halo_ctr[0] = 0
    sctr[0] = 0
    proj_ctr[0] = 0
    xg_ctr[0] = 0
    misc_ctr[0] = 0
    W.prime()
    WD.prime()
    program()
    s.emit(nc)
    return nc


_NC_CACHE = {}


def _get_nc(stage):
    if stage not in _NC_CACHE:
        _NC_CACHE[stage] = build(stage)
    return _NC_CACHE[stage]


def _consts():
    c = np.zeros((128, 3, 128), np.float32)
    c[:, 0, :] = 1.0
    jp = np.arange(128)[:, None]
    j = np.arange(128)[None, :]
    c[:, 1, :] = -(jp >= j).astype(np.float32)
    c[:, 2, :] = -1.0
    y = np.arange(896)[None, :]
    p = np.arange(128)[:, None]
    band = ((y - 384) > p).astype(np.float32)
    return c, band


def _common_maps(inp):
    cst, band = _consts()
    lng = np.ascontiguousarray(inp["ln_g"].reshape(6, KC, 128).transpose(2, 0, 1))
    lnb = np.ascontiguousarray(inp["ln_b"].reshape(6, KC, 128).transpose(2, 0, 1))
    return dict(cst=cst, lng=lng, lnb=lnb), band


def _maps_A(inp, common):
    x = inp["x"]
    maps = []
    pscale = np.ascontiguousarray(inp["pool_scale"][0].reshape(8, 128).T)
    convw = np.ascontiguousarray(inp["conv_w"][0].reshape(3, 8, 128).transpose(2, 1, 0))
    for core in range(NCORES):
        b, r = divmod(core, 4)
        xt = np.zeros((D, TH), np.float32)
        lo = r * T
        xt[:, HALO:] = x[b, lo:lo + T].T
        if r > 0:
            xt[:, :HALO] = x[b, lo - HALO:lo].T
        invc = np.zeros((128, 4, 16), np.float32)
        for g in range(4):
            win = 2 ** (g + 1)
            if r == 0:
                invc[:, g, :] = 1.0 / np.minimum(np.arange(1, 17), win)
            else:
                invc[:, g, :] = 1.0 / win
        m = dict(common)
        m.update(xT=xt, wgate=inp["ffn_w_gate"], wup=inp["ffn_w_up"], wdown=inp["ffn_w_down"],
                 wmixin=inp["mix_w_in"][0], wpool=inp["pool_w"][0], pscale=pscale, convw=convw,
                 hmask=np.full((128, 1), 0.0 if r == 0 else 1.0, np.float32), invc=invc,
                 wmixout=inp["mix_w_out"][0])
        maps.append(m)
    return maps


def _wqkv_core(inp, r):
    w = inp["attn_w_qkv"][0].reshape(D, 3, 16, 128)
    return np.ascontiguousarray(w[:, :, 4 * r:4 * r + 4, :])


def kernel(**inp):
    inp = {k: np.asarray(v) for k, v in inp.items()}
    common, band = _common_maps(inp)
    cores = list(range(NCORES))
    resA = run_bass_kernel_spmd(_get_nc("A"), _maps_A(inp, common), core_ids=cores).results
    mapsB = []
    for core in cores:
        b, r = divmod(core, 4)
        xg = np.concatenate([resA[4 * b + rr]["xb_out"] for rr in range(4)], axis=0)
        m = dict(common)
        m.update(xgath=xg, wqkv=_wqkv_core(inp, r), band=band)
        mapsB.append(m)
    resB = run_bass_kernel_spmd(_get_nc("B"), mapsB, core_ids=cores).results
    mapsC = []
    for core in cores:
        b, r = divmod(core, 4)
        om = np.concatenate([resB[4 * b + rr]["oT"][:, r * T:(r + 1) * T] for rr in range(4)], axis=0)
        m = dict(common)
        m.update(xf_in=resA[core]["xf_out"], om=np.ascontiguousarray(om), wgate=inp["ffn_w_gate"],
                 wup=inp["ffn_w_up"], wdown=inp["ffn_w_down"], wattout=inp["attn_w_out"][0])
        mapsC.append(m)
    resC = run_bass_kernel_spmd(_get_nc("C"), mapsC, core_ids=cores).results
    out = np.empty((2, SEQ, D), np.float32)
    for core in cores:
        b, r = divmod(core, 4)
        out[b, r * T:(r + 1) * T] = resC[core]["outT"].T
    return out
```

```python
import numpy as np
import ml_dtypes
from collections import namedtuple
from contextlib import ExitStack

import concourse.bass as bass
import concourse.mybir as mybir
from concourse.bass_utils import run_bass_kernel_spmd

F32 = mybir.dt.float32
BF16 = mybir.dt.bfloat16
AF = mybir.ActivationFunctionType
ALU = mybir.AluOpType

D = 2048
DFF = 5632
T = 1024
HALO = 16
TH = T + HALO
KC = D // 128
FC = DFF // 128
NG = 4
GC = FC // NG
SEQ = 4096
ALPHA = 4.0 ** 0.25
LN_EPS = 1e-5
QSCALE = 128.0 ** -0.5
NCORES = 8

SAME_ENGINE_SYNC = False
COMPUTE = ("pe", "act", "dve", "pool")
ALL_ENG = ("pe", "act", "dve", "pool", "sp")


class Res:
    __slots__ = ("w", "r")

    def __init__(self):
        self.w = None
        self.r = {}


class DmaSem:
    __slots__ = ("count", "handle", "name")

    def __init__(self, name):
        self.count = 0
        self.handle = None
        self.name = name


class _Op:
    __slots__ = ("emit", "waits_c", "waits_d", "signal", "sig_no", "dma")

    def __init__(self, emit, waits_c, waits_d, dma):
        self.emit = emit
        self.waits_c = waits_c
        self.waits_d = waits_d
        self.signal = False
        self.sig_no = 0
        self.dma = dma


class Sched:
    def __init__(self):
        self.streams = {e: [] for e in ALL_ENG}
        self.seen_c = {e: {} for e in ALL_ENG}
        self.seen_d = {e: {} for e in ALL_ENG}
        self.dma_sems = []
        self.plan = False
        self.same_sync = False

    def dma_sem(self, name):
        s = DmaSem(name)
        self.dma_sems.append(s)
        return s

    def op(self, eng, emit, reads=(), writes=(), dma=None, short=None):
        if self.plan:
            return None
        if short is None:
            short = self.same_sync
        deps_c = {}
        deps_d = {}

        def add(tok, raw):
            if tok is None:
                return
            if tok[0] == "c":
                if tok[1] == eng and not SAME_ENGINE_SYNC and not (raw and tok[3]):
                    return
                if tok[2] > deps_c.get(tok[1], -1):
                    deps_c[tok[1]] = tok[2]
            else:
                k = id(tok[1])
                if k not in deps_d or tok[2] > deps_d[k][1]:
                    deps_d[k] = (tok[1], tok[2])

        for r in reads:
            add(r.w, True)
        for w in writes:
            add(w.w, False)
            for t in w.r.values():
                add(t, False)
        waits_c = []
        sc = self.seen_c[eng]
        for e2, idx in deps_c.items():
            if idx > sc.get(e2, -1):
                sc[e2] = idx
                waits_c.append((e2, idx))
        waits_d = []
        sd = self.seen_d[eng]
        for k, (sem, val) in deps_d.items():
            if val > sd.get(k, 0):
                sd[k] = val
                waits_d.append((sem, val))
        stream = self.streams[eng]
        idx = len(stream)
        if dma is not None:
            dma.count += 16
            tok = ("d", dma, dma.count)
            key = id(dma)
        else:
            tok = ("c", eng, idx, bool(short))
            key = eng
        for r in reads:
            r.r[key] = tok
        for w in writes:
            w.w = tok
            w.r = {}
        stream.append(_Op(emit, waits_c, waits_d, dma))
        return tok

    def emit(self, nc):
        for eng in ALL_ENG:
            for op in self.streams[eng]:
                for (e2, idx) in op.waits_c:
                    self.streams[e2][idx].signal = True
        for eng in ALL_ENG:
            n = 0
            for op in self.streams[eng]:
                if op.signal:
                    n += 1
                    op.sig_no = n
        with ExitStack() as st:
            csem = {e: st.enter_context(nc.semaphore("s_" + e)) for e in COMPUTE}
            for s in self.dma_sems:
                if s.count > 0:
                    s.handle = st.enter_context(nc.semaphore("d_" + s.name))
            block = st.enter_context(nc.Block())
            streams = self.streams

            def run(eng_name, engine):
                for op in streams[eng_name]:
                    for (e2, idx) in op.waits_c:
                        engine.wait_ge(csem[e2], streams[e2][idx].sig_no)
                    for (sem, val) in op.waits_d:
                        engine.wait_ge(sem.handle, val)
                    ins = op.emit(engine)
                    if op.dma is not None:
                        ins.then_inc(op.dma.handle, 16)
                    elif op.signal:
                        ins.then_inc(csem[eng_name], 1)
                last = {}
                for op in streams[eng_name]:
                    if op.dma is not None:
                        last[id(op.dma)] = op.dma
                for s in last.values():
                    engine.wait_ge(s.handle, s.count)

            @block.tensor
            def _(e):
                run("pe", e)

            @block.scalar
            def _(e):
                run("act", e)

            @block.vector
            def _(e):
                run("dve", e)

            @block.gpsimd
            def _(e):
                run("pool", e)

            @block.sync
            def _(e):
                run("sp", e)


class Slot:
    __slots__ = ("t", "res", "sem")


class WStream:
    def __init__(self, K, name, slots):
        self.K = K
        self.name = name
        self.slots = slots
        self.specs = []
        self.fences = []
        self.reset()

    def reset(self):
        self.next_use = 0
        self.next_release = 0
        self.loaded = 0
        self.fence_passed = 0
        self.extra = []
        self.extra_reads = []

    def _load(self, i):
        slot = self.slots[i % len(self.slots)]
        for piece in self.specs[i]:
            dst_fn, src = piece
            self.K.s.op("pool", (lambda e, d=dst_fn(slot.t), s_=src: e.dma_start(out=d, in_=s_)),
                        reads=self.extra_reads, writes=[slot.res] + self.extra, dma=slot.sem)
        self.loaded = i + 1

    def _pump(self):
        limit = self.fences[self.fence_passed] if self.fence_passed < len(self.fences) else len(self.specs)
        hi = min(len(self.specs), self.next_release + len(self.slots), limit)
        while self.loaded < hi:
            self._load(self.loaded)

    def prime(self, n=None, after=()):
        self.extra_reads = list(after)
        if n is None:
            self._pump()
        else:
            while self.loaded < min(n, len(self.specs)):
                self._load(self.loaded)
        self.extra_reads = []

    def fence(self, extra_res):
        if self.K.s.plan:
            self.fences.append(len(self.specs))
            return
        self.fence_passed += 1
        self.extra = list(extra_res)
        self._pump()
        self.extra = []

    def acquire(self, spec):
        K = self.K
        if K.s.plan:
            self.specs.append(spec)
            return self.slots[0]
        i = self.next_use
        assert i < self.loaded, (self.name, i, self.loaded)
        self.next_use += 1
        return self.slots[i % len(self.slots)]

    def release(self):
        if self.K.s.plan:
            return
        self.next_release += 1
        self._pump()


TT = namedtuple("TT", "c0 n kind")
TT_H = TT(0, HALO, "h")
TT_0 = TT(HALO, 512, 0)
TT_1 = TT(HALO + 512, 512, 1)
TTS3 = (TT_H, TT_0, TT_1)
TTS2 = (TT_0, TT_1)


class Kern:
    pass


def build(stage, stop=None):
    nc = bass.Bass("TRN2", target_bir_lowering=False)
    K = Kern()
    K.nc = nc
    K.s = Sched()
    s = K.s
    doA = stage in ("A", "F")
    doB = stage in ("B", "F")
    doC = stage in ("C", "F")
    fused = stage == "F"

    def din(name, shape, dt=F32):
        return nc.dram_tensor(name, list(shape), dt, kind="ExternalInput").ap()

    _douts = {}

    def dout(name, shape, dt=F32):
        if name not in _douts:
            _douts[name] = nc.dram_tensor(name, list(shape), dt, kind="ExternalOutput").ap()
        return _douts[name]

    lng = din("lng", [128, 6, KC])
    lnb = din("lnb", [128, 6, KC])
    cst = din("cst", [128, 3, 128])
    if doA or doC:
        wgate = din("wgate", [2, 2, D, DFF])
        wup = din("wup", [2, 2, D, DFF])
        wdown = din("wdown", [2, 2, DFF, D])
    if doA:
        xT = din("xT", [D, TH])
        wmixin = din("wmixin", [D, 4096])
        wpool = din("wpool", [4, 256, 256])
        pscale = din("pscale", [128, 8])
        convw = din("convw", [128, 8, 3])
        hmask = din("hmask", [128, 1])
        invc = din("invc", [128, 4, 16])
        wmixout = din("wmixout", [D, D])
    if doB:
        wqkv = din("wqkv", [D, 3, 4, 128])
        band = din("band", [128, 896])
    if doC:
        wattout = din("wattout", [D, D])
    if fused:
        agin = nc.dram_tensor("agin", [D, T], BF16).ap()
        xg = nc.dram_tensor("xgath", [4 * D, T], BF16).ap()
        a2in = nc.dram_tensor("a2in", [4 * 512, T], BF16).ap()
        a2out = nc.dram_tensor("a2out", [4 * 512, T], BF16).ap()
        outT = dout("outT", [D, T])
    else:
        if stage == "A":
            xf_out = dout("xf_out", [D, T])
            xb_out = dout("xb_out", [D, T], BF16)
        if stage == "B":
            xg = din("xgath", [4 * D, T], BF16)
            oT = dout("oT", [512, SEQ], BF16)
        if stage == "C":
            xf_in = din("xf_in", [D, T])
            a2out = din("om", [D, T], BF16)
            outT = dout("outT", [D, T])

    base0 = (nc.sbuf_base + 31) // 32 * 32
    arena = nc.alloc_sbuf_tensor("arena", [128, (nc.sbuf_top - base0) // 32 * 32], mybir.dt.uint8)
    off = [base0]
    sb_limit = base0 + (nc.sbuf_top - base0) // 32 * 32

    def at(name, shape, dt, o=None, advance=True):
        nbytes = int(np.prod(shape[1:])) * (4 if dt == F32 else 2)
        if o is None:
            o = off[0]
            if advance:
                off[0] += (nbytes + 31) // 32 * 32
        return nc.alloc_sbuf_tensor_at(name, list(shape), dt, offset=o)

    XF = at("XF", [128, KC, TH], F32)
    o_xb = off[0]
    XB = at("XB", [128, KC, TH], BF16)
    o_w = off[0]
    wslots = []
    for i in range(4):
        sl = Slot()
        sl.t = at("W%d" % i, [128, KC, 256], BF16)
        sl.res = Res()
        sl.sem = s.dma_sem("w%d" % i)
        wslots.append(sl)
    o_rs = off[0]
    RS_SIZE = 49152
    off[0] += RS_SIZE
    AT = at("AT", [128, GC, TH], BF16, o=o_rs)
    wdslots = []
    for i in range(2):
        sl = Slot()
        sl.t = at("WD%d" % i, [128, GC, 512], BF16, o=o_rs + GC * TH * 2 + i * GC * 512 * 2)
        sl.res = Res()
        sl.sem = s.dma_sem("wd%d" % i)
        wdslots.append(sl)
    assert GC * TH * 2 + 2 * GC * 512 * 2 <= RS_SIZE
    YM = at("YM", [128, KC, T], BF16, o=o_rs)
    MT = [at("MT%d" % i, [128, TH], F32, o=o_rs + KC * T * 2 + i * TH * 4) for i in range(3)]
    assert KC * T * 2 + 3 * TH * 4 <= RS_SIZE
    QT = [at("QT%d" % i, [128, SEQ], BF16, o=o_rs + i * 24576) for i in range(2)]
    KT = [at("KT%d" % i, [128, SEQ], BF16, o=o_rs + i * 24576 + 8192) for i in range(2)]
    VV = [at("VV%d" % i, [128, 32, 128], BF16, o=o_rs + i * 24576 + 16384) for i in range(2)]
    XG = [at("XG%d" % i, [128, KC, 512], BF16, o=o_xb + i * 16384) for i in range(2)]
    o_tmp = off[0]
    off[0] += 8 * 2048
    TF = [at("TF%d" % i, [128, 512], F32, o=o_tmp + i * 2048) for i in range(8)]
    TB = [at("TB%d" % i, [128, 1024], BF16, o=o_tmp + i * 2048) for i in range(8)]
    PW = at("PW", [128, 4, 2, 256], BF16, o=o_tmp + 4 * 2048)
    rT_ = [Res() for _ in range(8)]
    rTh = [[Res(), Res()] for _ in range(8)]

    def RW(i):
        return (rT_[i], rTh[i][0], rTh[i][1])
    CST = at("CST", [128, 3, 128], BF16)
    LNG = at("LNG", [128, 6, KC], F32)
    LNB = at("LNB", [128, 6, KC], F32)
    LNAB = at("LNAB", [128, 6, KC], F32)
    EPS = at("EPS", [128, 2], F32)
    if doA:
        PSC = at("PSC", [128, 8], F32)
        CVW = at("CVW", [128, 8, 3], F32)
        HM = at("HM", [128, 1], F32)
        INVC = at("INVC", [128, 4, 16], F32)
    if doB:
        BAND = at("BAND", [128, 896], BF16)
    assert off[0] <= sb_limit, (off[0], sb_limit)
    ONES = CST[:, 0, :]
    NTRI = CST[:, 1, :]
    NONES = CST[:, 2, :]

    pb = [nc.alloc_psum_tensor("pb%d" % i, [128, 512], F32) for i in range(8)]
    rpb = [Res() for _ in range(8)]
    rhalo = [Res() for _ in range(32)]
    halo_ctr = [0]

    def halo_slot():
        i = halo_ctr[0] % 32
        halo_ctr[0] += 1
        return pb[4][:, 16 * i:16 * i + 16], rhalo[i]

    rXF = [[Res() for _ in range(3)] for _ in range(KC)]
    rXB = [[Res() for _ in range(3)] for _ in range(KC)]
    rAT = [[Res() for _ in range(3)] for _ in range(GC)]
    rYM = [[Res() for _ in range(2)] for _ in range(KC)]
    rMT = [Res() for _ in range(3)]
    rC = Res()
    rPW = Res()

    def ti(tt):
        return 0 if tt.kind == "h" else 1 + tt.kind

    def cols(tt):
        return slice(tt.c0, tt.c0 + tt.n)

    W = WStream(K, "W", wslots)
    WD = WStream(K, "WD", wdslots)

    def wspec_cols(src2d, c0, n, dst0=0):
        return ((lambda t, a=dst0, b=n: t[:, :, a:a + b]),
                src2d[:, c0:c0 + n].rearrange("(c p) f -> p c f", p=128))

    def mm_group(e, out_ap, pairs):
        n = len(pairs)
        ins = None
        for i, (l, r) in enumerate(pairs):
            ins = e.matmul(out_ap, lhsT=l, rhs=r, start=(i == 0), stop=(i == n - 1))
        return ins

    dsem_misc = [s.dma_sem("m%d" % i) for i in range(8)]
    misc_ctr = [0]

    def misc_sem():
        i = misc_ctr[0] % 8
        misc_ctr[0] += 1
        return dsem_misc[i]

    def setup():
        s.op("pool", lambda e: e.dma_start(out=CST[:], in_=cst), writes=[rC], dma=misc_sem())
        s.op("sp", lambda e: e.dma_start(out=LNG[:], in_=lng), writes=[rC], dma=misc_sem())
        s.op("sp", lambda e: e.dma_start(out=LNB[:], in_=lnb), writes=[rC], dma=misc_sem())
        s.op("dve", lambda e: e.memset(EPS[:, 0:1], LN_EPS), writes=[rC])
        s.op("dve", lambda e: e.memset(EPS[:, 1:2], 1.0), writes=[rC])
        s.op("dve", lambda e: e.tensor_scalar(out=LNAB[:], in0=LNB[:], scalar1=ALPHA, scalar2=None, op0=ALU.mult),
             reads=[rC], writes=[rC])
        if doA:
            s.op("sp", lambda e: e.dma_start(out=PSC[:], in_=pscale), writes=[rC], dma=misc_sem())
            s.op("sp", lambda e: e.dma_start(out=CVW[:], in_=convw), writes=[rC], dma=misc_sem())
            s.op("sp", lambda e: e.dma_start(out=HM[:], in_=hmask), writes=[rC], dma=misc_sem())
            s.op("sp", lambda e: e.dma_start(out=INVC[:], in_=invc), writes=[rC], dma=misc_sem())
        if doB:
            s.op("pool", lambda e: e.dma_start(out=BAND[:], in_=band), writes=[rC], dma=misc_sem())

    sctr = [0]

    def ffn(l, j, tts):
        wg = wgate[l, j]
        wu = wup[l, j]
        wd = wdown[l, j]
        cur = {}
        f2ctr = [0]
        for g in range(NG):
            for fl in range(GC):
                f = g * GC + fl
                bi, half = divmod(f, 2)
                if half == 0 or not cur:
                    cur["g"] = W.acquire([wspec_cols(wg, bi * 256, 256)])
                    cur["u"] = W.acquire([wspec_cols(wu, bi * 256, 256)])
                sg, su = cur["g"], cur["u"]
                cs = slice(half * 128, half * 128 + 128)
                for tt in tts:
                    s.same_sync = tt.kind == "h"
                    if tt.kind == "h":
                        ph, rh = halo_slot()
                        pu, ru = halo_slot()
                    else:
                        ph, rh = pb[tt.kind][:], rpb[tt.kind]
                        pu, ru = pb[2 + tt.kind][:], rpb[2 + tt.kind]
                    xr = [rXB[c][ti(tt)] for c in range(KC)]

                    def mm(e, ph=ph, pu=pu, sg=sg, su=su, cs=cs, tt=tt):
                        mm_group(e, ph, [(sg.t[:, c, cs], XB[:, c, cols(tt)]) for c in range(KC)])
                        return mm_group(e, pu, [(su.t[:, c, cs], XB[:, c, cols(tt)]) for c in range(KC)])
                    s.op("pe", mm, reads=xr + [sg.res, su.res], writes=[rh, ru])
                    si = sctr[0] % 2
                    sctr[0] += 1
                    S = TF[si][:, 0:tt.n]
                    s.op("act", lambda e, S=S, ph=ph: e.activation(out=S, in_=ph, func=AF.Silu),
                         reads=[rh], writes=[*RW(si)])
                    s.op("dve", lambda e, S=S, pu=pu, fl=fl, tt=tt: e.tensor_tensor(
                        out=AT[:, fl, cols(tt)], in0=S, in1=pu, op=ALU.mult),
                        reads=[*RW(si), ru], writes=[rAT[fl][ti(tt)]])
                if half == 1 or f == FC - 1:
                    W.release()
                    W.release()
                    cur.clear()
            for db in range(4):
                sd = WD.acquire([((lambda t: t[:]),
                                  wd[g * GC * 128:(g + 1) * GC * 128, db * 512:(db + 1) * 512].rearrange(
                                      "(f p) d -> p f d", p=128))])
                for dl in range(4):
                    dc = db * 4 + dl
                    for tt in tts:
                        s.same_sync = tt.kind == "h"
                        if tt.kind == "h":
                            pf, rf = halo_slot()
                        else:
                            bk = f2ctr[0] % 4
                            f2ctr[0] += 1
                            pf, rf = pb[bk][:], rpb[bk]
                        ar = [rAT[fl][ti(tt)] for fl in range(GC)]
                        s.op("pe", lambda e, pf=pf, sd=sd, dl=dl, tt=tt: mm_group(
                            e, pf, [(sd.t[:, fl, dl * 128:(dl + 1) * 128], AT[:, fl, cols(tt)]) for fl in range(GC)]),
                            reads=ar + [sd.res], writes=[rf])
                        s.op("dve", lambda e, pf=pf, dc=dc, tt=tt: e.scalar_tensor_tensor(
                            out=XF[:, dc, cols(tt)], in0=pf, scalar=0.5, in1=XF[:, dc, cols(tt)],
                            op0=ALU.mult, op1=ALU.add),
                            reads=[rf, rXF[dc][ti(tt)]], writes=[rXF[dc][ti(tt)]])
                WD.release()
        s.same_sync = False

    def layer_norm(li, tts, final=False):
        for tt in tts:
            s.same_sync = tt.kind == "h"
            layer_norm_tt(li, tt, final)
        s.same_sync = False

    def layer_norm_tt(li, tt, final):
        if True:
            n = tt.n
            psum_, rsum = pb[5][:, 0:n], rpb[5]
            psq_, rsq = pb[6][:, 0:n], rpb[6]
            tcol = cols(tt)
            for c in range(KC):
                hb = c % 2
                yb = TB[2][:, hb * 512:hb * 512 + n]
                sq = TB[3][:, hb * 512:hb * 512 + n]
                s.op("dve", lambda e, yb=yb, c=c: e.tensor_copy(out=yb, in_=XF[:, c, tcol]),
                     reads=[rXF[c][ti(tt)]], writes=[rTh[2][hb]])
                s.op("act", lambda e, sq=sq, c=c: e.activation(out=sq, in_=XF[:, c, tcol], func=AF.Square),
                     reads=[rXF[c][ti(tt)]], writes=[rTh[3][hb]])

                def mm(e, yb=yb, sq=sq, c=c):
                    e.matmul(psum_, lhsT=ONES, rhs=yb, start=(c == 0), stop=(c == KC - 1))
                    return e.matmul(psq_, lhsT=ONES, rhs=sq, start=(c == 0), stop=(c == KC - 1))
                wr = [rsum] if rsum is rsq else [rsum, rsq]
                s.op("pe", mm, reads=[rTh[2][hb], rTh[3][hb], rC], writes=wr)
            M = TF[4][:, 0:n]
            V = TF[1][:, 0:n]
            R = TF[5][:, 0:n]
            NM = TF[6][:, 0:n]
            s.op("dve", lambda e: e.tensor_scalar(out=M, in0=psum_, scalar1=1.0 / D, scalar2=None, op0=ALU.mult),
                 reads=[rsum], writes=[*RW(4)])
            s.op("dve", lambda e: e.tensor_tensor(out=V, in0=M, in1=M, op=ALU.mult), reads=[*RW(4)], writes=[*RW(1)])
            s.op("dve", lambda e: e.scalar_tensor_tensor(out=V, in0=psq_, scalar=1.0 / D, in1=V,
                                                         op0=ALU.mult, op1=ALU.subtract),
                 reads=[rsq, *RW(1)], writes=[*RW(1)])
            s.op("act", lambda e: e.activation(out=V, in_=V, func=AF.Ln, bias=EPS[:, 0:1], scale=1.0),
                 reads=[*RW(1), rC], writes=[*RW(1)])
            s.op("act", lambda e: e.activation(out=R, in_=V, func=AF.Exp, scale=-0.5), reads=[*RW(1)], writes=[*RW(5)])
            s.op("dve", lambda e: e.scalar_tensor_tensor(out=NM, in0=M, scalar=-1.0, in1=R, op0=ALU.mult, op1=ALU.mult),
                 reads=[*RW(4), *RW(5)], writes=[*RW(6)])
            for c in range(KC):
                tsl = 7 if c % 2 == 0 else 0
                t1 = TF[tsl][:, 0:n]
                gc = LNG[:, li, c:c + 1]
                s.op("dve", lambda e, t1=t1, c=c, gc=gc: e.scalar_tensor_tensor(
                    out=t1, in0=XF[:, c, tcol], scalar=gc, in1=R, op0=ALU.mult, op1=ALU.mult),
                    reads=[rXF[c][ti(tt)], *RW(5), rC], writes=[*RW(tsl)])
                s.op("dve", lambda e, t1=t1, gc=gc: e.scalar_tensor_tensor(
                    out=t1, in0=NM, scalar=gc, in1=t1, op0=ALU.mult, op1=ALU.add),
                    reads=[*RW(6), *RW(tsl), rC], writes=[*RW(tsl)])
                if final:
                    s.op("act", lambda e, t1=t1, c=c: e.activation(
                        out=XF[:, c, tcol], in_=t1, func=AF.Identity, bias=LNB[:, li, c:c + 1], scale=1.0),
                        reads=[*RW(tsl), rC], writes=[rXF[c][ti(tt)]])
                else:
                    s.op("act", lambda e, t1=t1, c=c: e.activation(
                        out=XB[:, c, tcol], in_=t1, func=AF.Identity, bias=LNB[:, li, c:c + 1], scale=1.0),
                        reads=[*RW(tsl), rC], writes=[rXB[c][ti(tt)]])
                    s.op("act", lambda e, t1=t1, c=c: e.activation(
                        out=XF[:, c, tcol], in_=t1, func=AF.Identity, bias=LNAB[:, li, c:c + 1], scale=ALPHA),
                        reads=[*RW(tsl), rC], writes=[rXF[c][ti(tt)]])

    def proj_chunk(slot, cs, tts, want_halo=True):
        outs = []
        pair = proj_ctr[0] % 2
        proj_ctr[0] += 1
        for tt in tts:
            if tt.kind == "h":
                p, r = halo_slot()
            else:
                b = pair * 2 + tt.kind
                p, r = pb[b][:], rpb[b]
            xr = [rXB[c][ti(tt)] for c in range(KC)]
            s.op("pe", lambda e, p=p, slot=slot, cs=cs, tt=tt: mm_group(
                e, p, [(slot.t[:, c, cs], XB[:, c, cols(tt)]) for c in range(KC)]),
                reads=xr + [slot.res], writes=[r])
            outs.append((tt, p, r))
        return outs

    proj_ctr = [0]

    def mixer0():
        mw = wmixin
        s.op("pool", lambda e: e.dma_start(out=PW[:], in_=wpool.rearrange("g (i p) c -> p g i c", p=128)),
             reads=[], writes=[rPW, *RW(4), *RW(5)], dma=misc_sem())
        U, A_, B_ = MT[0], MT[1], MT[2]
        rU, rA, rB = rMT
        for g in range(4):
            win = 2 ** (g + 1)
            slot = W.acquire([wspec_cols(mw, 256 * g, 256)])
            for i in range(2):
                ch = 2 * g + i
                outs = proj_chunk(slot, slice(i * 128, i * 128 + 128), TTS3)
                for (tt, p, r) in outs:
                    if tt.kind == "h":
                        s.op("act", lambda e, p=p: e.activation(out=U[:, 0:16], in_=p, func=AF.Identity,
                                                                scale=HM[:, 0:1]),
                             reads=[r, rC], writes=[rU])
                    else:
                        s.op("act", lambda e, p=p, tt=tt: e.activation(out=U[:, cols(tt)], in_=p, func=AF.Identity),
                             reads=[r], writes=[rU])
                src, rsrc = U, rU
                st = 1
                pp = [(A_, rA), (B_, rB)]
                k = 0
                while st < win:
                    dst, rdst = pp[k % 2]
                    lo = 2 * st - 1
                    s.op("dve", lambda e, dst=dst, src=src, lo=lo, st=st: e.tensor_tensor(
                        out=dst[:, lo:TH], in0=src[:, lo:TH], in1=src[:, lo - st:TH - st], op=ALU.add),
                        reads=[rsrc], writes=[rdst])
                    src, rsrc = dst, rdst
                    st *= 2
                    k += 1
                DM = TB[2 + i]
                s.op("dve", lambda e, src=src, DM=DM, win=win: e.scalar_tensor_tensor(
                    out=DM[:, 0:T], in0=src[:, HALO:TH], scalar=1.0 / win, in1=U[:, HALO:TH],
                    op0=ALU.mult, op1=ALU.subtract),
                    reads=[rsrc, rU], writes=[*RW(2 + i)])
                oth, roth = pp[k % 2]
                s.op("dve", lambda e, src=src, oth=oth, g=g: e.tensor_tensor(
                    out=oth[:, 0:16], in0=src[:, HALO:HALO + 16], in1=INVC[:, g, :], op=ALU.mult),
                    reads=[rsrc, rC], writes=[roth])
                s.op("dve", lambda e, oth=oth, DM=DM: e.tensor_tensor(
                    out=DM[:, 0:16], in0=oth[:, 0:16], in1=U[:, HALO:HALO + 16], op=ALU.subtract),
                    reads=[roth, rU], writes=[*RW(2 + i)])
            W.release()
            for jj in range(2):
                ch = 2 * g + jj
                for tt in TTS2:
                    b = 5 + tt.kind
                    s.op("pe", lambda e, b=b, g=g, jj=jj, tt=tt: mm_group(
                        e, pb[b][:], [(PW[:, g, i2, jj * 128:(jj + 1) * 128],
                                       TB[2 + i2][:, tt.c0 - HALO:tt.c0 - HALO + 512]) for i2 in range(2)]),
                        reads=[rPW, *RW(2), *RW(3)], writes=[rpb[b]])
                    s.op("act", lambda e, b=b, ch=ch, tt=tt: e.activation(
                        out=YM[:, ch, tt.c0 - HALO:tt.c0 - HALO + 512], in_=pb[b][:], func=AF.Identity,
                        scale=PSC[:, ch:ch + 1]),
                        reads=[rpb[b], rC], writes=[rYM[ch][tt.kind]])
        for ip in range(4):
            slC = W.acquire([wspec_cols(mw, 2048 + 256 * ip, 256)])
            slX = W.acquire([wspec_cols(mw, 3072 + 256 * ip, 256)])
            slB = W.acquire([wspec_cols(mw, 1024 + 256 * ip, 256)])
            for i in range(2):
                ci = 2 * ip + i
                cs = slice(i * 128, i * 128 + 128)
                Csb, rCs = MT[0], rMT[0]
                Z, rZ = MT[1], rMT[1]
                ACC, rACC = MT[2], rMT[2]
                for (tt, p, r) in proj_chunk(slC, cs, TTS3):
                    if tt.kind == "h":
                        s.op("act", lambda e, p=p: e.activation(out=Csb[:, 0:16], in_=p, func=AF.Identity,
                                                                scale=HM[:, 0:1]),
                             reads=[r, rC], writes=[rCs])
                    else:
                        s.op("act", lambda e, p=p, tt=tt: e.activation(out=Csb[:, cols(tt)], in_=p, func=AF.Identity),
                             reads=[r], writes=[rCs])
                for (tt, p, r) in proj_chunk(slX, cs, TTS3):
                    s.op("dve", lambda e, p=p, tt=tt: e.tensor_tensor(
                        out=Z[:, cols(tt)], in0=Csb[:, cols(tt)], in1=p, op=ALU.mult),
                        reads=[r, rCs], writes=[rZ])
                s.op("dve", lambda e, ci=ci: e.tensor_scalar(
                    out=ACC[:, 0:T], in0=Z[:, HALO:TH], scalar1=CVW[:, ci, 2:3], scalar2=None, op0=ALU.mult),
                    reads=[rZ, rC], writes=[rACC])
                s.op("dve", lambda e, ci=ci: e.scalar_tensor_tensor(
                    out=ACC[:, 0:T], in0=Z[:, HALO - 1:TH - 1], scalar=CVW[:, ci, 1:2], in1=ACC[:, 0:T],
                    op0=ALU.mult, op1=ALU.add),
                    reads=[rZ, rC, rACC], writes=[rACC])
                s.op("dve", lambda e, ci=ci: e.scalar_tensor_tensor(
                    out=ACC[:, 0:T], in0=Z[:, HALO - 2:TH - 2], scalar=CVW[:, ci, 0:1], in1=ACC[:, 0:T],
                    op0=ALU.mult, op1=ALU.add),
                    reads=[rZ, rC, rACC], writes=[rACC])
                for (tt, p, r) in proj_chunk(slB, cs, TTS2):
                    s.op("dve", lambda e, p=p, tt=tt, ci=ci: e.tensor_tensor(
                        out=YM[:, 8 + ci, tt.c0 - HALO:tt.c0 - HALO + 512],
                        in0=ACC[:, tt.c0 - HALO:tt.c0 - HALO + 512], in1=p, op=ALU.mult),
                        reads=[r, rACC], writes=[rYM[8 + ci][tt.kind]])
            W.release()
            W.release()
            W.release()
        out_proj(wmixout, YM, rYM)

    def out_proj(wsrc, Y, rY):
        for ob in range(8):
            slot = W.acquire([wspec_cols(wsrc, 256 * ob, 256)])
            for i in range(2):
                dc = 2 * ob + i
                pair = proj_ctr[0] % 2
                proj_ctr[0] += 1
                for tt in TTS2:
                    b = pair * 2 + tt.kind
                    s.op("pe", lambda e, b=b, slot=slot, i=i, tt=tt: mm_group(
                        e, pb[b][:], [(slot.t[:, c, i * 128:(i + 1) * 128],
                                       Y[:, c, tt.c0 - HALO:tt.c0 - HALO + 512]) for c in range(KC)]),
                        reads=[rY[c][tt.kind] for c in range(KC)] + [slot.res], writes=[rpb[b]])
                    s.op("dve", lambda e, b=b, dc=dc, tt=tt: e.tensor_tensor(
                        out=XF[:, dc, cols(tt)], in0=pb[b][:], in1=XF[:, dc, cols(tt)], op=ALU.add),
                        reads=[rpb[b], rXF[dc][ti(tt)]], writes=[rXF[dc][ti(tt)]])
            W.release()

    rXG = [Res(), Res()]
    xg_sem = [s.dma_sem("xg0"), s.dma_sem("xg1")]
    rQ = [[Res() for _ in range(8)] for _ in range(2)]
    rK = [[Res() for _ in range(8)] for _ in range(2)]
    rV = [[Res() for _ in range(8)] for _ in range(2)]
    xg_ctr = [0]
    rXGdram = Res()
    rOdram = Res()
    osem = [s.dma_sem("o0"), s.dma_sem("o1")]

    def proj_items(hl, j, slQK, slV):
        buf = hl % 2
        st = {}

        def load():
            xi = xg_ctr[0] % 2
            xg_ctr[0] += 1
            st["xi"] = xi
            st["xs"] = XG[xi]
            xs = XG[xi]
            src_ = xg[(j // 2) * D:(j // 2 + 1) * D, (j % 2) * 512:(j % 2) * 512 + 512].rearrange(
                "(c p) t -> p c t", p=128)
            s.op("sp", lambda e: e.dma_start(out=xs[:], in_=src_),
                 reads=[rXGdram],
                 writes=[rXG[xi]] + ([rXB[c][k] for c in range(KC) for k in range(3)] if xg_ctr[0] <= 2 else []),
                 dma=xg_sem[xi])

        def grp(bank, cs, lo, hi):
            xs, xi = st["xs"], st["xi"]

            def mm(e):
                ins = None
                for c in range(lo, hi):
                    ins = e.matmul(pb[bank][:], lhsT=slQK.t[:, c, cs], rhs=xs[:, c, :], start=(c == 0), stop=(c == KC - 1))
                return ins
            s.op("pe", mm, reads=[rXG[xi], slQK.res], writes=[rpb[bank]])

        def q1():
            load()
            grp(6, slice(0, 128), 0, 8)

        def q2():
            grp(6, slice(0, 128), 8, 16)
            s.op("act", lambda e: e.activation(out=QT[buf][:, j * 512:(j + 1) * 512], in_=pb[6][:], func=AF.Identity,
                                               scale=QSCALE),
                 reads=[rpb[6]], writes=[rQ[buf][j]])

        def k1():
            grp(7, slice(128, 256), 0, 8)

        def k2():
            grp(7, slice(128, 256), 8, 16)
            s.op("dve", lambda e: e.tensor_copy(out=KT[buf][:, j * 512:(j + 1) * 512], in_=pb[7][:]),
                 reads=[rpb[7]], writes=[rK[buf][j]])

        def vblk(blk):
            def f():
                xs, xi = st["xs"], st["xi"]
                s.op("pe", lambda e: mm_group(e, pb[6][:, blk * 128:(blk + 1) * 128],
                                              [(xs[:, c, blk * 128:(blk + 1) * 128], slV.t[:, c, 0:128]) for c in range(KC)]),
                     reads=[rXG[xi], slV.res], writes=[rpb[6]])
                if blk == 3:
                    s.op("dve", lambda e: e.tensor_copy(out=VV[buf][:, 4 * j:4 * j + 4, :],
                                                        in_=pb[6][:].rearrange("p (b d) -> p b d", b=4)),
                         reads=[rpb[6]], writes=[rV[buf][j]])
            return f
        return [q1, q2, k1, k2, vblk(0), vblk(1), vblk(2), vblk(3)]

    def proj_tile(hl, j, slQK, slV):
        for it in proj_items(hl, j, slQK, slV):
            it()

    SPB = [(TB[2], 0, rTh[2][0]), (TB[2], 1, rTh[2][1]), (TB[6], 0, rTh[6][0]), (TB[6], 1, rTh[6][1])]
    ABF = [(TB[4], 0, rTh[4][0]), (TB[4], 1, rTh[4][1]), (TB[7], 0, rTh[7][0]), (TB[7], 1, rTh[7][1])]

    def half(t, h):
        return t[:, h * 512:h * 512 + 512]

    def attention():
        def get_w(hl):
            a = W.acquire([((lambda t: t[:, :, 0:128]), wqkv[:, 0, hl, :].rearrange("(c p) f -> p c f", p=128)),
                           ((lambda t: t[:, :, 128:256]), wqkv[:, 1, hl, :].rearrange("(c p) f -> p c f", p=128))])
            b = W.acquire([((lambda t: t[:, :, 0:128]), wqkv[:, 2, hl, :].rearrange("(c p) f -> p c f", p=128))])
            return a, b
        pairs = []
        for hl in range(4):
            for jq in range(8):
                nkb = 4 * jq + 4
                for kb in range(nkb - 1, -1, -1):
                    pairs.append((hl, jq, kb, kb == nkb - 1, kb == 0))
        N = len(pairs)
        cbi = {}
        wsl = {}

        def info(n):
            hl, jq, kb, first, last = pairs[n]
            buf = hl % 2
            diag = kb >= 4 * jq
            dd = kb - 4 * jq
            mask = BAND[:, 384 - 128 * dd:384 - 128 * dd + 512] if diag else None
            return hl, jq, kb, first, last, buf, mask

        def st_z(n):
            hl, jq, kb, first, last, buf, mask = info(n)
            zb = n % 2
            ks = KT[buf][:, kb * 128:(kb + 1) * 128]
            qs = QT[buf][:, jq * 512:(jq + 1) * 512]
            spt, sph, rSP = SPB[n % 4]
            SP = half(spt, sph)
            E = TF[zb]
            s.op("pe", lambda e: e.matmul(pb[zb][:], lhsT=ks, rhs=qs, start=True, stop=True),
                 reads=[rK[buf][kb // 4], rQ[buf][jq]], writes=[rpb[zb]])
            s.op("act", lambda e: e.activation(out=E[:], in_=pb[zb][:], func=AF.Exp),
                 reads=[rpb[zb]], writes=[*RW(zb)])
            s.op("act", lambda e: e.activation(out=SP, in_=E[:], func=AF.Ln, bias=EPS[:, 1:2], scale=1.0),
                 reads=[*RW(zb), rC], writes=[rSP])
            if mask is not None:
                s.op("dve", lambda e: e.tensor_tensor(out=SP, in0=SP, in1=mask, op=ALU.mult),
                     reads=[rSP, rC], writes=[rSP])

        def st_la(n):
            hl, jq, kb, first, last, buf, mask = info(n)
            lb = 2 + n % 2
            ks = KT[buf][:, kb * 128:(kb + 1) * 128]
            qs = QT[buf][:, jq * 512:(jq + 1) * 512]
            spt, sph, rSP = SPB[n % 4]
            SP = half(spt, sph)
            at_, ah, rA = ABF[n % 4]
            A = half(at_, ah)
            cb = None if first else cbi[n]

            def mml(e):
                e.matmul(pb[lb][:], lhsT=ks, rhs=qs, start=True, stop=False)
                ins = e.matmul(pb[lb][:], lhsT=NTRI, rhs=SP, start=False, stop=first)
                if not first:
                    ins = e.matmul(pb[lb][:], lhsT=NONES, rhs=half(TB[3], cb), start=False, stop=True)
                return ins
            rd = [rK[buf][kb // 4], rQ[buf][jq], rSP, rC]
            if not first:
                rd.append(rTh[3][cb])
            s.op("pe", mml, reads=rd, writes=[rpb[lb]])
            if not last:
                if first:
                    s.op("dve", lambda e: e.tensor_copy(out=half(TB[3], 0), in_=SP), reads=[rSP], writes=[rTh[3][0]])
                    cbi[n + 1] = 0
                else:
                    nx = 1 - cb
                    s.op("dve", lambda e: e.tensor_tensor(out=half(TB[3], nx), in0=half(TB[3], cb), in1=SP, op=ALU.add),
                         reads=[rSP, rTh[3][cb]], writes=[rTh[3][nx]])
                    cbi[n + 1] = nx
            s.op("act", lambda e: e.activation(out=A, in_=pb[lb][:], func=AF.Exp), reads=[rpb[lb]], writes=[rA])
            if mask is not None:
                s.op("dve", lambda e: e.tensor_tensor(out=A, in0=A, in1=mask, op=ALU.mult),
                     reads=[rA, rC], writes=[rA])

        def st_av(n):
            hl, jq, kb, first, last, buf, mask = info(n)
            ob = 4 + (jq % 2)
            at_, ah, rA = ABF[n % 4]
            A = half(at_, ah)
            s.op("pe", lambda e: e.matmul(pb[ob][:], lhsT=VV[buf][:, kb, :], rhs=A, start=first, stop=last),
                 reads=[rV[buf][kb // 4], rA], writes=[rpb[ob]])
            if last:
                osl = jq % 2
                OS = half(TB[5], osl)
                s.op("dve", lambda e: e.tensor_copy(out=OS, in_=pb[ob][:]), reads=[rpb[ob]], writes=[rTh[5][osl]])
                if fused:
                    r_ = jq // 2
                    dst = a2in[r_ * 512 + hl * 128:r_ * 512 + (hl + 1) * 128, (jq % 2) * 512:(jq % 2) * 512 + 512]
                else:
                    dst = oT[hl * 128:(hl + 1) * 128, jq * 512:(jq + 1) * 512]
                s.op("sp", lambda e: e.dma_start(out=dst, in_=OS), reads=[rTh[5][osl]], writes=[rOdram], dma=osem[osl])

        a, b = get_w(0)
        proj_tile(0, 0, a, b)
        pending = []
        for j in range(1, 8):
            pending.extend((0, j, it) for it in proj_items(0, j, a, b))
        pending.append((0, 7, lambda: (W.release(), W.release())))

        def flush_upto(hl_, j_):
            while pending and (pending[0][0], pending[0][1]) <= (hl_, j_):
                pending.pop(0)[2]()

        for k in range(-2, N):
            if 0 <= k + 2 < N:
                hl2, jq2, kb2, first2, last2 = pairs[k + 2]
                if first2:
                    flush_upto(hl2, jq2)
                st_z(k + 2)
            if 0 <= k + 1 < N:
                st_la(k + 1)
            if 0 <= k < N:
                hl, jq, kb, first, last = pairs[k]
                st_av(k)
                if first and hl + 1 < 4:
                    if jq == 0:
                        wsl[hl + 1] = get_w(hl + 1)
                    pending.extend((hl + 1, jq, it) for it in proj_items(hl + 1, jq, *wsl[hl + 1]))
                    if jq == 7:
                        pending.append((hl + 1, 7, lambda: (W.release(), W.release())))
                if pending and (hl == 0 or k % 2 == 1):
                    pending.pop(0)[2]()

    cc_sem = [s.dma_sem("cc0"), s.dma_sem("cc1")]
    GROUPS = [[0, 1, 2, 3], [4, 5, 6, 7]]

    def program():
        setup()
        allxf = [rXF[c][k] for c in range(KC) for k in range(3)]
        allxb = [rXB[c][k] for c in range(KC) for k in range(3)]
        if not s.plan:
            if doA or stage == "C":
                W.prime(2)
            else:
                W.prime()
        if doA:
            s.op("sp", lambda e: e.dma_start(out=XF[:], in_=xT.rearrange("(c p) t -> p c t", p=128)),
                 writes=allxf, dma=misc_sem())
            if not s.plan:
                W.prime(after=[rXF[0][1]])
                WD.prime(after=[rXF[0][1]])
            for c in range(KC):
                s.op("dve", lambda e, c=c: e.tensor_copy(out=XB[:, c, :], in_=XF[:, c, :]),
                     reads=rXF[c], writes=rXB[c])
                s.op("act", lambda e, c=c: e.activation(out=XF[:, c, :], in_=XF[:, c, :], func=AF.Identity, scale=ALPHA),
                     reads=rXF[c] + rXB[c], writes=rXF[c])
            def dbg():
                dx = dout("dbg_xf", [D, TH])
                db = dout("dbg_xb", [D, TH], BF16)
                s.op("sp", lambda e: e.dma_start(out=dx.rearrange("(c p) t -> p c t", p=128), in_=XF[:]),
                     reads=allxf, dma=misc_sem())
                s.op("sp", lambda e: e.dma_start(out=db.rearrange("(c p) t -> p c t", p=128), in_=XB[:]),
                     reads=allxb, dma=misc_sem())
            if stop == "init":
                return dbg()
            ffn(0, 0, TTS3)
            if stop == "ffn":
                return dbg()
            layer_norm(0, TTS3)
            if stop == "ln0":
                return dbg()
            s.same_sync = True
            mixer0()
            s.same_sync = False
            if stop == "mixer":
                return dbg()
            def dump(name):
                if stop != "all":
                    return
                dx = dout("dump_" + name, [D, TH])
                s.op("sp", lambda e: e.dma_start(out=dx.rearrange("(c p) t -> p c t", p=128), in_=XF[:]),
                     reads=allxf, dma=misc_sem())
            WD.fence([r for c in range(KC) for r in rYM[c]] + rMT)
            layer_norm(1, TTS2)
            dump("ln1")
            ffn(0, 1, TTS2)
            dump("ffn01")
            layer_norm(2, TTS2)
            dump("ln2")
            ffn(1, 0, TTS2)
            dump("ffn10")
            layer_norm(3, TTS2)
            if fused:
                s.op("sp", lambda e: e.dma_start(out=agin.rearrange("(c p) t -> p c t", p=128), in_=XB[:, :, HALO:TH]),
                     reads=allxb, writes=[rXGdram], dma=misc_sem())
                s.op("pool", lambda e: e.collective_compute("AllGather", ALU.bypass, replica_groups=GROUPS,
                                                            ins=[agin], outs=[xg]),
                     reads=[rXGdram], writes=[rXGdram], dma=cc_sem[0])
            else:
                s.op("sp", lambda e: e.dma_start(out=xb_out.rearrange("(c p) t -> p c t", p=128), in_=XB[:, :, HALO:TH]),
                     reads=allxb, dma=misc_sem())
                s.op("sp", lambda e: e.dma_start(out=xf_out.rearrange("(c p) t -> p c t", p=128), in_=XF[:, :, HALO:TH]),
                     reads=allxf, dma=misc_sem())
        if doB:
            attention()
        if doC:
            OM = XB[:, :, HALO:TH]
            rOM = [[rXB[c][1], rXB[c][2]] for c in range(KC)]
            if fused:
                s.op("pool", lambda e: e.collective_compute("AllToAll", ALU.bypass, replica_groups=GROUPS,
                                                            ins=[a2in], outs=[a2out]),
                     reads=[rOdram], writes=[rOdram], dma=cc_sem[1])
            else:
                s.op("sp", lambda e: e.dma_start(out=XF[:, :, HALO:TH], in_=xf_in.rearrange("(c p) t -> p c t", p=128)),
                     writes=allxf, dma=misc_sem())
            s.op("sp", lambda e: e.dma_start(out=OM, in_=a2out.rearrange("(c p) t -> p c t", p=128)),
                 reads=[rOdram], writes=allxb + rXG, dma=misc_sem())
            if not s.plan and not fused:
                W.prime(after=[rXB[0][1], rXF[0][1]])
                WD.prime(after=[rXB[0][1], rXF[0][1]])

            class _Y:
                def __getitem__(self, idx):
                    p, c, ts = idx
                    return XB[:, c, HALO + ts.start:HALO + ts.stop]
            WD.fence([r for b_ in range(2) for j_ in range(8) for r in (rQ[b_][j_], rK[b_][j_], rV[b_][j_])])
            out_proj(wattout, _Y(), rOM)
            layer_norm(4, TTS2)
            ffn(1, 1, TTS2)
            layer_norm(5, TTS2, final=True)
            for tt in TTS2:
                s.op("sp", lambda e, tt=tt: e.dma_start(
                    out=outT.rearrange("(c p) t -> p c t", p=128)[:, :, tt.c0 - HALO:tt.c0 - HALO + 512],
                    in_=XF[:, :, cols(tt)]),
                    reads=[rXF[c][ti(tt)] for c in range(KC)], dma=misc_sem())

    s.plan = True
    program()
    s.plan = False
    W.reset()
    WD.reset()
    halo_ctr[0] = 0
    sctr[0] = 0
    proj_ctr[0] = 0
    xg_ctr[0] = 0
    misc_ctr[0] = 0
    program()
    s.emit(nc)
    return nc


_NC_CACHE = {}


def _get_nc(stage):
    if stage not in _NC_CACHE:
        _NC_CACHE[stage] = build(stage)
    return _NC_CACHE[stage]


def _consts():
    c = np.zeros((128, 3, 128), np.float32)
    c[:, 0, :] = 1.0
    jp = np.arange(128)[:, None]
    j = np.arange(128)[None, :]
    c[:, 1, :] = -(jp >= j).astype(np.float32)
    c[:, 2, :] = -1.0
    y = np.arange(896)[None, :]
    p = np.arange(128)[:, None]
    band = ((y - 384) > p).astype(np.float32)
    return c, band


def _common_maps(inp):
    cst, band = _consts()
    lng = np.ascontiguousarray(inp["ln_g"].reshape(6, KC, 128).transpose(2, 0, 1))
    lnb = np.ascontiguousarray(inp["ln_b"].reshape(6, KC, 128).transpose(2, 0, 1))
    return dict(cst=cst, lng=lng, lnb=lnb), band


def _maps_A(inp, common):
    x = inp["x"]
    maps = []
    pscale = np.ascontiguousarray(inp["pool_scale"][0].reshape(8, 128).T)
    convw = np.ascontiguousarray(inp["conv_w"][0].reshape(3, 8, 128).transpose(2, 1, 0))
    for core in range(NCORES):
        b, r = divmod(core, 4)
        xt = np.zeros((D, TH), np.float32)
        lo = r * T
        xt[:, HALO:] = x[b, lo:lo + T].T
        if r > 0:
            xt[:, :HALO] = x[b, lo - HALO:lo].T
        invc = np.zeros((128, 4, 16), np.float32)
        for g in range(4):
            win = 2 ** (g + 1)
            if r == 0:
                invc[:, g, :] = 1.0 / np.minimum(np.arange(1, 17), win)
            else:
                invc[:, g, :] = 1.0 / win
        m = dict(common)
        m.update(xT=xt, wgate=inp["ffn_w_gate"], wup=inp["ffn_w_up"], wdown=inp["ffn_w_down"],
                 wmixin=inp["mix_w_in"][0], wpool=inp["pool_w"][0], pscale=pscale, convw=convw,
                 hmask=np.full((128, 1), 0.0 if r == 0 else 1.0, np.float32), invc=invc,
                 wmixout=inp["mix_w_out"][0])
        maps.append(m)
    return maps


def _wqkv_core(inp, r):
    w = inp["attn_w_qkv"][0].reshape(D, 3, 16, 128)
    return np.ascontiguousarray(w[:, :, 4 * r:4 * r + 4, :])


def kernel(**inp):
    inp = {k: np.asarray(v) for k, v in inp.items()}
    common, band = _common_maps(inp)
    cores = list(range(NCORES))
    resA = run_bass_kernel_spmd(_get_nc("A"), _maps_A(inp, common), core_ids=cores).results
    mapsB = []
    for core in cores:
        b, r = divmod(core, 4)
        xg = np.concatenate([resA[4 * b + rr]["xb_out"] for rr in range(4)], axis=0)
        m = dict(common)
        m.update(xgath=xg, wqkv=_wqkv_core(inp, r), band=band)
        mapsB.append(m)
    resB = run_bass_kernel_spmd(_get_nc("B"), mapsB, core_ids=cores).results
    mapsC = []
    for core in cores:
        b, r = divmod(core, 4)
        om = np.concatenate([resB[4 * b + rr]["oT"][:, r * T:(r + 1) * T] for rr in range(4)], axis=0)
        m = dict(common)
        m.update(xf_in=resA[core]["xf_out"], om=np.ascontiguousarray(om), wgate=inp["ffn_w_gate"],
                 wup=inp["ffn_w_up"], wdown=inp["ffn_w_down"], wattout=inp["attn_w_out"][0])
        mapsC.append(m)
    resC = run_bass_kernel_spmd(_get_nc("C"), mapsC, core_ids=cores).results
    out = np.empty((2, SEQ, D), np.float32)
    for core in cores:
        b, r = divmod(core, 4)
        out[b, r * T:(r + 1) * T] = resC[core]["outT"].T
    return out
```
